# Optimizing a Trainium2 kernel written in Bass

```python
import jax, jax.numpy as jnp
from jax import lax
import numpy as np

D_MODEL = 1024
BATCH = 8
SEQ = 4096
DEPTH = 4

N_MIXERS = 2
N_LAYERS_A = (DEPTH + 1) // 2
N_LAYERS_B = DEPTH // 2
MIX_WIDTH = D_MODEL
CONV_WIDTH = 3
CONV_GROUPS = 8
POOL_WINDOWS = (2, 4, 8, 16)
N_POOL_GROUPS = len(POOL_WINDOWS)
POOL_GROUP_DIM = MIX_WIDTH // N_POOL_GROUPS
PLE_DIM = 256
EPS = 1e-6

kernel_name = "hybrid_shortconv_pool_ple_trunk"


def rmsnorm(x, g):
    xf = x.astype(jnp.float32)
    r = lax.rsqrt(jnp.mean(xf * xf, axis=-1, keepdims=True) + EPS)
    return (xf * r).astype(x.dtype) * g


def causal_conv3(u, w):
    s = u.shape[1]
    up = jnp.pad(u, ((0, 0), (CONV_WIDTH - 1, 0), (0, 0)))
    return up[:, 0:s] * w[0] + up[:, 1:s + 1] * w[1] + up[:, 2:s + 2] * w[2]


def short_conv_mixer(h, w_in, w_conv, w_out):
    proj = h @ w_in
    b_g, c_g, v, z = jnp.split(proj, 4, axis=-1)
    y = b_g * causal_conv3(c_g * v, w_conv)
    return (jax.nn.silu(z) * y) @ w_out


def causal_window_mean(u, window):
    s = u.shape[1]
    cs = jnp.cumsum(u.astype(jnp.float32), axis=1)
    csp = jnp.pad(cs, ((0, 0), (window, 0), (0, 0)))
    win_sum = csp[:, window:window + s] - csp[:, 0:s]
    count = jnp.minimum(jnp.arange(1, s + 1, dtype=jnp.float32), float(window))
    return (win_sum / count[None, :, None]).astype(u.dtype)


def pool_mixer(h, w_in, w_grp, scale, w_out):
    proj = h @ w_in
    u, z = jnp.split(proj, 2, axis=-1)
    bsz, s, _ = u.shape
    u4 = u.reshape(bsz, s, N_POOL_GROUPS, POOL_GROUP_DIM)
    pooled = jnp.stack([causal_window_mean(u4[:, :, g], w) for g, w in enumerate(POOL_WINDOWS)], axis=2)
    d = pooled - u4
    mixed = jnp.einsum('bsgc,gcd->bsgd', d, w_grp).reshape(bsz, s, MIX_WIDTH) * scale
    return (jax.nn.silu(z) * mixed) @ w_out


def setup_inputs(seed: int = 0) -> dict:
    key = jax.random.key(seed)
    ks = jax.random.split(key, 16)
    f32 = jnp.float32
    E, D, G = MIX_WIDTH, D_MODEL, POOL_GROUP_DIM
    nrm = lambda k, shape, fan_in: jax.random.normal(k, shape, f32) * (fan_in ** -0.5)
    gain = lambda k, shape: 1.0 + 0.02 * jax.random.normal(k, shape, f32)
    return {
        "x": jax.random.normal(ks[0], (BATCH, SEQ, D), f32),
        "p": jax.random.normal(ks[1], (DEPTH, BATCH, SEQ, PLE_DIM), f32),
        "norm_mix": gain(ks[2], (DEPTH, D)),
        "a_w_in": nrm(ks[3], (N_LAYERS_A, D, 4 * E), D),
        "a_w_conv": nrm(ks[4], (N_LAYERS_A, CONV_WIDTH, E), CONV_WIDTH),
        "a_w_out": nrm(ks[5], (N_LAYERS_A, E, D), E),
        "b_w_in": nrm(ks[6], (N_LAYERS_B, D, 2 * E), D),
        "b_w_grp": nrm(ks[7], (N_LAYERS_B, N_POOL_GROUPS, G, G), G),
        "b_scale": gain(ks[8], (N_LAYERS_B, E)),
        "b_w_out": nrm(ks[9], (N_LAYERS_B, E, D), E),
        "ple_norm": gain(ks[10], (DEPTH, D)),
        "ple_w_gate": nrm(ks[11], (DEPTH, D, D), D),
        "ple_w_proj": nrm(ks[12], (DEPTH, PLE_DIM, D), PLE_DIM),
        "final_norm": gain(ks[13], (D,)),
    }


def reference(x, p, norm_mix, a_w_in, a_w_conv, a_w_out, b_w_in, b_w_grp, b_scale, b_w_out,
              ple_norm, ple_w_gate, ple_w_proj, final_norm):
    h = x
    for i in range(DEPTH):
        hn = rmsnorm(h, norm_mix[i])
        j = i // N_MIXERS
        if i % N_MIXERS == 0:
            h = h + short_conv_mixer(hn, a_w_in[j], a_w_conv[j], a_w_out[j])
        else:
            h = h + pool_mixer(hn, b_w_in[j], b_w_grp[j], b_scale[j], b_w_out[j])
        gate = jax.nn.sigmoid(rmsnorm(h, ple_norm[i]) @ ple_w_gate[i])
        h = h + gate * (p[i] @ ple_w_proj[i])
    return rmsnorm(h, final_norm)
```

```python
import numpy as np
from contextlib import ExitStack

import concourse.bass as bass
import concourse.mybir as mybir
from concourse.bass_utils import run_bass_kernel_spmd

F32 = mybir.dt.float32
BF16 = mybir.dt.bfloat16
AF = mybir.ActivationFunctionType
ALU = mybir.AluOpType

D = 1024
SEQ = 4096
DEPTH = 4
PLE = 256
EPS = 1e-6
KC = 8
TW = 512
NT = 2
SUP = TW * NT
POOL_WINDOWS = (2, 4, 8, 16)
HB = 16

PC_NMIX = 0
PC_NPLE = 32
PC_NFIN = 64
PC_CONV = 72
PC_SCALE = 120
NPAR = 136
NG32 = 72


class Sched:
    ENGS = ("pe", "act", "dve", "pool", "sp")

    def __init__(self):
        self.ops = []
        self.last_w = {}
        self.readers = {}
        self.eng_ops = {e: [] for e in self.ENGS}
        self.dma_keys = []

    def add(self, eng, fn, reads=(), writes=(), dma=None, ninst=1):
        oid = len(self.ops)
        deps = set()
        for r in reads:
            w = self.last_w.get(r)
            if w is not None:
                deps.add(w)
        for w_ in writes:
            w = self.last_w.get(w_)
            if w is not None:
                deps.add(w)
            for rd in self.readers.get(w_, ()):
                deps.add(rd)
        deps.discard(oid)
        for r in reads:
            self.readers.setdefault(r, []).append(oid)
        for w_ in writes:
            self.last_w[w_] = oid
            self.readers[w_] = []
        op = dict(id=oid, eng=eng, fn=fn, deps=deps, dma=dma, ninst=ninst,
                  seq=len(self.eng_ops[eng]), signal=False, tok=None)
        if dma is not None and dma not in self.dma_keys:
            self.dma_keys.append(dma)
        self.ops.append(op)
        self.eng_ops[eng].append(op)
        return oid

    def _needed(self, op, d):
        if d["dma"] is not None:
            return True
        if d["eng"] != op["eng"]:
            return True
        if op["eng"] == "pe":
            return False
        return True

    def finalize(self):
        for op in self.ops:
            for did in op["deps"]:
                d = self.ops[did]
                if d["dma"] is None and self._needed(op, d):
                    d["signal"] = True
        cnt = {e: 0 for e in self.ENGS}
        dcnt = {k: 0 for k in self.dma_keys}
        for op in self.ops:
            if op["dma"] is not None:
                dcnt[op["dma"]] += 16 * op["ninst"]
                op["tok"] = (("dma", op["dma"]), dcnt[op["dma"]])
            elif op["signal"]:
                cnt[op["eng"]] += 1
                op["tok"] = (("eng", op["eng"]), cnt[op["eng"]])
        self.final_dma = dict(dcnt)
        self.final_cnt = dict(cnt)

    def emit_engine(self, eng, e, sems):
        waited = {}
        attach = eng != "pe"
        for op in self.eng_ops[eng]:
            need = {}
            for did in op["deps"]:
                d = self.ops[did]
                if not self._needed(op, d):
                    continue
                key, val = d["tok"]
                if waited.get(key, 0) >= val:
                    continue
                if need.get(key, 0) < val:
                    need[key] = val
            items = list(need.items())
            sep = items[:-1] if attach else items
            for key, val in sep:
                e.wait_ge(sems[key], val)
                waited[key] = val
            insts = op["fn"](e)
            if not isinstance(insts, (list, tuple)):
                insts = [insts]
            if attach and items:
                key, val = items[-1]
                insts[0]._wait_ge(sems[key], val)
                waited[key] = val
            if op["dma"] is not None:
                assert len(insts) == op["ninst"], (len(insts), op["ninst"])
                for ins in insts:
                    ins.then_inc(sems[("dma", op["dma"])], 16)
            elif op["signal"]:
                insts[-1].then_inc(sems[("eng", eng)], 1)
        return waited


def chunk_order(i):
    return list(range(KC)) if i % 2 == 0 else [6, 7, 4, 5, 2, 3, 0, 1]


def build_program(nsup=SEQ // SUP, ntok=SEQ, debug=()):
    nc = bass.Bass("TRN2", target_bir_lowering=False)
    S = Sched()
    if debug:
        dbg_d = nc.dram_tensor("dbg", [len(debug), 128, KC * SUP], F32, kind="ExternalOutput").ap()

    x_d = nc.dram_tensor("x", [ntok, D], F32, kind="ExternalInput").ap()
    p_d = nc.dram_tensor("p", [DEPTH, ntok, PLE], F32, kind="ExternalInput").ap()
    par_d = nc.dram_tensor("params", [128, NPAR], F32, kind="ExternalInput").ap()
    awin_d = nc.dram_tensor("a_w_in", [2, D, 4 * D], F32, kind="ExternalInput").ap()
    awout_d = nc.dram_tensor("a_w_out", [2, D, D], F32, kind="ExternalInput").ap()
    bwin_d = nc.dram_tensor("b_w_in", [2, D, 2 * D], F32, kind="ExternalInput").ap()
    bwgrp_d = nc.dram_tensor("b_w_grp", [2, 4, 256, 256], F32, kind="ExternalInput").ap()
    bwout_d = nc.dram_tensor("b_w_out", [2, D, D], F32, kind="ExternalInput").ap()
    wgate_d = nc.dram_tensor("ple_w_gate", [DEPTH, D, D], F32, kind="ExternalInput").ap()
    wproj_d = nc.dram_tensor("ple_w_proj", [DEPTH, PLE, D], F32, kind="ExternalInput").ap()
    gfin_d = nc.dram_tensor("gfin", [1, D], F32, kind="ExternalInput").ap()
    y_d = nc.dram_tensor("y", [ntok, D], F32, kind="ExternalOutput").ap()

    es = ExitStack()

    def sb(name, shape, dt):
        return es.enter_context(nc.sbuf_tensor(name, shape, dt))

    NSZ = 8
    ND = 8
    hT = sb("hT", [128, KC, SUP], F32)
    hn = sb("hn", [128, KC, SUP], BF16)
    mT = sb("mT", [128, KC, SUP], BF16)
    win = sb("win", [128, 3, KC, 4, 128], BF16)
    wout = sb("wout", [128, KC, D], BF16)
    wgate = sb("wgate", [128, KC, D], BF16)
    wproj = sb("wproj", [128, 2, D], BF16)
    wgrp = sb("wgrp", [128, 4, 2, 256], BF16)
    pT = sb("pT", [128, NT, 2, TW], BF16)
    pst = sb("pst", [128, SUP // 128, PLE], BF16)
    sq = sb("sq", [128, KC, TW], BF16)
    rr = sb("rr", [128, 2, TW], F32)
    szb = sb("szb", [128, NSZ, TW], F32)
    cgb = sb("cgb", [128, 2, TW], F32)
    cvb = sb("cvb", [128, 2, HB + TW], F32)
    yb = sb("yb", [128, 2, TW], F32)
    tb = sb("tb", [128, 2, TW], F32)
    tmpb = sb("tmpb", [128, 2, TW], F32)
    sab = sb("sab", [128, 2, HB + TW], F32)
    dT = sb("dT", [128, ND, TW], BF16)
    fix = sb("fix", [128, 16], F32)
    cbias = sb("cbias", [128, 1], F32)
    par = sb("par", [128, NPAR], F32)
    ident = sb("ident", [128, 128], F32)
    identb = sb("identb", [128, 128], BF16)
    ones = sb("ones", [128, 128], BF16)
    invc = sb("invc", [128, 5, 16], F32)
    halo_a = sb("halo_a", [128, 2, KC, 2], F32)
    halo_b = sb("halo_b", [128, 2, KC, 16], F32)
    gbc = sb("gbc", [128, D], F32)
    junk = sb("junk", [128, D], BF16)
    ssq = sb("ssq", [128, 8], F32)
    rt = sb("rt", [128, 8], F32)
    cst = sb("cst", [128, 2], F32)
    ps = es.enter_context(nc.psum_tensor("ps", [128, 8, TW], F32))

    xst = mT[:].bitcast(F32)
    ost = szb[:, :, :].rearrange("p (s a) t -> p s (a t)", a=2)

    def xst_slot(s):
        return xst[:, 2 * s:2 * s + 2, :].rearrange("p a b -> p (a b)")

    def xst_keys(s):
        return [("m", 2 * s + a, j) for a in (0, 1) for j in range(NT)]

    def ost_keys(s):
        return [("sz", 2 * s), ("sz", 2 * s + 1)]

    def dump(name, kind="h"):
        if name not in debug:
            return
        idx = list(debug).index(name)
        src, key = {"h": (hT, "h"), "hn": (hn, "hn"), "m": (mT, "m")}[kind]
        dst = dbg_d[idx] if kind == "h" else dbg_d[idx].bitcast(BF16)[:, 0:KC * SUP]
        S.add("sp", lambda e: e.dma_start(out=dst, in_=src[:].rearrange("p k t -> p (k t)")),
              reads=[(key, k, j) for k in range(KC) for j in range(NT)], writes=[("dbg", idx)], dma=("dbg", idx))

    ring = {"n": 0}

    def bank():
        b = ring["n"] % 8
        ring["n"] += 1
        return b

    def bank2():
        if ring["n"] % 2:
            ring["n"] += 1
        b = ring["n"] % 8
        ring["n"] += 2
        return b

    def pair(b):
        return ps[:, b:b + 2, :].rearrange("p a b -> p (a b)")

    def tsl(j):
        return slice(j * TW, (j + 1) * TW)

    def pcol(c):
        return par[:, c:c + 1]

    S.add("sp", lambda e: e.dma_start(out=par[:], in_=par_d[:, :]), writes=[("par",)], dma="par")
    S.add("pool", lambda e: e.memset(ident[:], 0.0), writes=[("ident",)])
    S.add("pool", lambda e: e.affine_select(out=ident[:], in_=ident[:], compare_op=ALU.not_equal, fill=1.0,
                                            base=0, pattern=[[-1, 128]], channel_multiplier=1),
          reads=[("ident",)], writes=[("ident",)])
    S.add("pool", lambda e: e.tensor_copy(out=identb[:], in_=ident[:]), reads=[("ident",)], writes=[("identb",)])
    S.add("pool", lambda e: e.memset(ones[:], 1.0), writes=[("ones",)])
    S.add("pool", lambda e: e.memset(halo_a[:], 0.0), writes=[("halo_a", a, c) for a in range(2) for c in range(KC)])
    S.add("pool", lambda e: e.memset(halo_b[:], 0.0), writes=[("halo_b", a, c) for a in range(2) for c in range(KC)])

    def mk_inv(t):
        return lambda e: e.memset(invc[:, 4, t:t + 1], 1.0 / (t + 1))
    for t in range(16):
        S.add("pool", mk_inv(t), writes=[("invc",)])

    def mk_invw(wi, w):
        return lambda e: e.tensor_scalar(out=invc[:, wi, :], in0=invc[:, 4, :], scalar1=1.0 / w, scalar2=None,
                                         op0=ALU.max)
    for wi, w in enumerate(POOL_WINDOWS):
        S.add("dve", mk_invw(wi, w), reads=[("invc",)], writes=[("invcw", wi)])
    S.add("pool", lambda e: e.memset(cbias[:], float(D * EPS)), writes=[("cbias",)])
    S.add("sp", lambda e: e.dma_start(out=gbc[:], in_=gfin_d[0:1, :].to_broadcast([128, D])),
          writes=[("gbc",)], dma="gbc")
    S.add("dve", lambda e: e.tensor_scalar(out=gbc[:], in0=gbc[:], scalar1=32.0, scalar2=None, op0=ALU.mult),
          reads=[("gbc",)], writes=[("gbc",)])
    S.add("pool", lambda e: e.memset(cst[:, 0:1], float(D * EPS)), writes=[("cst",)])
    S.add("pool", lambda e: e.memset(cst[:, 1:2], -0.5), writes=[("cst",)])
    S.add("dve", lambda e: e.tensor_scalar(out=par[:, 0:NG32], in0=par[:, 0:NG32], scalar1=32.0, scalar2=None,
                                           op0=ALU.mult), reads=[("par",)], writes=[("par",)])

    pieces = []
    for s_ in range(nsup):
        for i in range(DEPTH):
            for c in chunk_order(i):
                pieces.append((i, c))
    st = {"next_piece": 0}

    def load_piece(n):
        i, c = pieces[n]
        slot = n % 3
        if i % 2 == 0:
            src = awin_d[i // 2].rearrange("(k p) (g c e) -> p k g c e", p=128, g=4, c=KC)
            ng = 4
        else:
            src = bwin_d[i // 2].rearrange("(k p) (g c e) -> p k g c e", p=128, g=2, c=KC)
            ng = 2

        def fn(e):
            return [e.dma_start(out=win[:, slot, :, g, :], in_=src[:, :, g, c, :]) for g in range(ng)]
        S.add("pool", fn, writes=[("win", slot)], dma=("win", slot), ninst=ng)

    def ensure_pieces(upto):
        while st["next_piece"] <= min(upto, len(pieces) - 1):
            load_piece(st["next_piece"])
            st["next_piece"] += 1

    def load_p(i, sup):
        src = p_d[i, sup * SUP:(sup + 1) * SUP, :].rearrange("(n p) d -> p n d", p=128)
        S.add("pool", lambda e: e.dma_start(out=pst[:, :, :], in_=src), writes=[("pst",)], dma=("pst",))

    def loads_wout(i):
        wo_src = (awout_d[i // 2] if i % 2 == 0 else bwout_d[i // 2]).rearrange("(k p) n -> p k n", p=128)
        if i % 2 == 1:
            def fn(e):
                return [e.dma_start(out=wgrp[:, g, :, :],
                                    in_=bwgrp_d[i // 2, g].rearrange("(ic p) n -> p ic n", p=128))
                        for g in range(4)]
            S.add("pool", fn, writes=[("wgrp",)], dma=("wgrp",), ninst=4)
        for h in range(2):
            ks = slice(4 * h, 4 * h + 4)
            S.add("pool", lambda e, ks=ks: e.dma_start(out=wout[:, ks, :], in_=wo_src[:, ks, :]),
                  writes=[("wout", h)], dma=("wout", h))

    def small_loads(i, sup):
        out = []
        wg_src = wgate_d[i].rearrange("(k p) n -> p k n", p=128)
        wp_src = wproj_d[i].rearrange("(k p) n -> p k n", p=128)
        for h in range(2):
            cs = slice(512 * h, 512 * h + 512)
            out.append(lambda cs=cs, h=h: S.add(
                "pool", lambda e: e.dma_start(out=wgate[:, :, cs], in_=wg_src[:, :, cs]),
                writes=[("wgate", h)], dma=("wgate", h)))
        out.append(lambda: S.add("pool", lambda e: e.dma_start(out=wproj[:], in_=wp_src),
                                 writes=[("wproj",)], dma=("wproj",)))
        return out

    def load_p_next(i, sup):
        ni, nsup_ = (i + 1, sup) if i + 1 < DEPTH else (0, sup + 1)
        if nsup_ < nsup:
            load_p(ni, nsup_)

    def load_wout_next(i, sup):
        ni, nsup_ = (i + 1, sup) if i + 1 < DEPTH else (0, sup + 1)
        if nsup_ < nsup:
            loads_wout(ni)

    def load_x_block(sup, blk):
        slot = blk % 4
        r0 = sup * SUP + blk * 128
        S.add("sp", lambda e: e.dma_start(out=xst_slot(slot), in_=x_d[r0:r0 + 128, :]),
              writes=xst_keys(slot), dma=("xst", slot))

    def stage_X(sup, j, blocks=(0, 1, 2, 3)):
        for n in blocks:
            blk = j * 4 + n
            slot = blk % 4
            b = bank2()
            pv = pair(b).rearrange("p (k t) -> p k t", t=128)
            for k in range(KC):
                S.add("pe", lambda e, k=k, pv=pv, slot=slot: e.transpose(
                    out=pv[:, k, :], in_=xst_slot(slot)[:, k * 128:(k + 1) * 128], identity=ident[:]),
                    reads=xst_keys(slot) + [("ident",)], writes=[("ps", b), ("ps", b + 1)])
            c0 = j * TW + n * 128
            S.add("act", lambda e, pv=pv, c0=c0: e.activation(out=hT[:, :, c0:c0 + 128], in_=pv, func=AF.Copy),
                  reads=[("ps", b), ("ps", b + 1)], writes=[("h", k, j) for k in range(KC)])
            if blk + 4 < SUP // 128:
                load_x_block(sup, blk + 4)

    nst = {"r": 0}

    def norm_sq(j):
        for k in range(KC):
            S.add("act", lambda e, k=k: e.activation(out=sq[:, k, :], in_=hT[:, k, tsl(j)], func=AF.Square),
                  reads=[("h", k, j)], writes=[("sq", k)])

    def norm_stats(j):
        b = bank()
        for k in range(KC):
            S.add("pe", lambda e, k=k: e.matmul(ps[:, b, :], lhsT=ones[:], rhs=sq[:, k, :],
                                                start=(k == 0), stop=(k == KC - 1)),
                  reads=[("sq", k), ("ones",)], writes=[("ps", b)])
        rs = nst["r"] % 2
        nst["r"] += 1
        S.add("act", lambda e: e.activation(out=rr[:, rs, :], in_=ps[:, b, :], func=AF.Ln, bias=cbias[:, 0:1],
                                            scale=1.0),
              reads=[("ps", b), ("cbias",)], writes=[("rr", rs)])
        S.add("act", lambda e: e.activation(out=rr[:, rs, :], in_=rr[:, rs, :], func=AF.Exp, scale=-0.5),
              reads=[("rr", rs)], writes=[("rr", rs)])
        return rs

    def norm_apply(j, rs, gcol, dst, dst_keys):
        for k in range(KC):
            S.add("dve", lambda e, k=k: e.scalar_tensor_tensor(
                out=dst(k), in0=hT[:, k, tsl(j)], scalar=pcol(gcol + k), in1=rr[:, rs, :],
                op0=ALU.mult, op1=ALU.mult),
                reads=[("h", k, j), ("rr", rs), ("par",)], writes=dst_keys(k))

    class Norm:
        def __init__(self, j, gcol):
            self.j, self.gcol = j, gcol

        def a(self):
            norm_sq(self.j)

        def b1(self):
            self.rs = norm_stats(self.j)

        def b2(self):
            j, rs = self.j, self.rs
            norm_apply(j, rs, self.gcol, (lambda k: hn[:, k, tsl(j)]), (lambda k: [("hn", k, j)]))

        def b(self):
            self.b1()
            self.b2()

        def both(self):
            self.a()
            self.b()

    cnt = {"sz": 0, "cg": 0, "cv": 0, "y": 0, "t": 0, "tmp": 0, "d": 0}

    def nxt(name, n):
        v = cnt[name] % n
        cnt[name] += 1
        return v

    def stage_P(i, j):
        b = bank()
        pv = ps[:, b, :].bitcast(BF16).rearrange("p (k t) -> p k t", k=2)
        for k2 in range(2):
            for n in range(4):
                S.add("pe", lambda e, k2=k2, n=n: e.transpose(
                    out=pv[:, k2, n * 128:(n + 1) * 128], in_=pst[:, j * 4 + n, k2 * 128:(k2 + 1) * 128],
                    identity=identb[:]),
                    reads=[("pst",), ("identb",)], writes=[("ps", b)])
        S.add("act", lambda e: e.activation(out=pT[:, j, :, :], in_=pv, func=AF.Copy),
              reads=[("ps", b)], writes=[("pT", j)])

    def mm_group(banks, lhs_of, j, kouter):
        ng = len(banks)
        order = [(g, k) for k in range(KC) for g in range(ng)] if kouter else \
                [(g, k) for g in range(ng) for k in range(KC)]
        for g, k in order:
            lhs, rd = lhs_of(g, k)
            S.add("pe", lambda e, g=g, k=k, lhs=lhs: e.matmul(ps[:, banks[g], :], lhsT=lhs, rhs=hn[:, k, tsl(j)],
                                                             start=(k == 0), stop=(k == KC - 1)),
                  reads=rd + [("hn", k, j)], writes=[("ps", banks[g])])

    def stage_A(i, c, j, slot, kouter=False):
        al = i // 2
        bs = [bank() for _ in range(4)]
        gmap = (1, 2, 0, 3)
        mm_group(bs, lambda g, k: (win[:, slot, k, gmap[g], :], [("win", slot)]), j, kouter)
        b_cg, b_v, b_bg, b_z = bs
        zs = nxt("sz", NSZ)
        gs = nxt("cg", 2)
        vs = nxt("cv", 2)
        ys = nxt("y", 2)
        w0 = PC_CONV + (al * 3 + 0) * KC + c
        w1 = PC_CONV + (al * 3 + 1) * KC + c
        w2 = PC_CONV + (al * 3 + 2) * KC + c
        S.add("act", lambda e: e.activation(out=cgb[:, gs, :], in_=ps[:, b_cg, :], func=AF.Copy),
              reads=[("ps", b_cg)], writes=[("cg", gs)])
        S.add("act", lambda e: e.activation(out=cvb[:, vs, HB - 2:HB], in_=halo_a[:, al, c, :], func=AF.Copy),
              reads=[("halo_a", al, c)], writes=[("cvh", vs)])
        S.add("dve", lambda e: e.tensor_tensor(out=cvb[:, vs, HB:HB + TW], in0=cgb[:, gs, :], in1=ps[:, b_v, :],
                                               op=ALU.mult),
              reads=[("cg", gs), ("ps", b_v)], writes=[("cv", vs)])
        S.add("act", lambda e: e.activation(out=halo_a[:, al, c, :], in_=cvb[:, vs, HB + TW - 2:HB + TW],
                                            func=AF.Copy),
              reads=[("cv", vs)], writes=[("halo_a", al, c)])
        S.add("act", lambda e: e.activation(out=yb[:, ys, :], in_=cvb[:, vs, HB - 2:HB - 2 + TW], func=AF.Identity,
                                            scale=pcol(w0)),
              reads=[("cv", vs), ("cvh", vs), ("par",)], writes=[("y", ys)])
        S.add("dve", lambda e: e.scalar_tensor_tensor(out=yb[:, ys, :], in0=cvb[:, vs, HB - 1:HB - 1 + TW],
                                                      scalar=pcol(w1), in1=yb[:, ys, :], op0=ALU.mult, op1=ALU.add),
              reads=[("cv", vs), ("cvh", vs), ("y", ys), ("par",)], writes=[("y", ys)])
        S.add("dve", lambda e: e.scalar_tensor_tensor(out=yb[:, ys, :], in0=cvb[:, vs, HB:HB + TW],
                                                      scalar=pcol(w2), in1=yb[:, ys, :], op0=ALU.mult, op1=ALU.add),
              reads=[("cv", vs), ("y", ys), ("par",)], writes=[("y", ys)])
        S.add("act", lambda e: e.activation(out=szb[:, zs, :], in_=ps[:, b_z, :], func=AF.Silu),
              reads=[("ps", b_z)], writes=[("sz", zs)])
        S.add("dve", lambda e: e.tensor_tensor(out=szb[:, zs, :], in0=szb[:, zs, :], in1=ps[:, b_bg, :],
                                               op=ALU.mult),
              reads=[("sz", zs), ("ps", b_bg)], writes=[("sz", zs)])
        S.add("dve", lambda e: e.tensor_tensor(out=mT[:, c, tsl(j)], in0=szb[:, zs, :], in1=yb[:, ys, :],
                                               op=ALU.mult),
              reads=[("sz", zs), ("y", ys)], writes=[("m", c, j)])

    def stage_B_uz(i, c, j, slot, first_tile, kouter=False):
        bl = i // 2
        w = POOL_WINDOWS[c // 2]
        bu, bz = bank(), bank()
        mm_group([bu, bz], lambda g, k: (win[:, slot, k, g, :], [("win", slot)]), j, kouter)
        zs = nxt("sz", NSZ)
        us = nxt("cv", 2)
        ds = nxt("d", ND)
        S.add("act", lambda e: e.activation(out=cvb[:, us, 1:HB], in_=halo_b[:, bl, c, 0:15], func=AF.Copy),
              reads=[("halo_b", bl, c)], writes=[("cvh", us)])
        S.add("act", lambda e: e.activation(out=cvb[:, us, HB:HB + TW], in_=ps[:, bu, :], func=AF.Copy),
              reads=[("ps", bu)], writes=[("cv", us)])
        S.add("act", lambda e: e.activation(out=halo_b[:, bl, c, 0:15], in_=cvb[:, us, HB + TW - 15:HB + TW],
                                            func=AF.Copy),
              reads=[("cv", us)], writes=[("halo_b", bl, c)])
        S.add("act", lambda e: e.activation(out=szb[:, zs, :], in_=ps[:, bz, :], func=AF.Silu),
              reads=[("ps", bz)], writes=[("sz", zs)])
        nstg = {2: 1, 4: 2, 8: 3, 16: 4}[w]
        prev, prev_keys = (lambda lo, hi: cvb[:, us, lo:hi]), [("cv", us), ("cvh", us)]
        for st_ in range(1, nstg + 1):
            lo = HB - (w - 2 ** st_)
            sh = 2 ** (st_ - 1)
            pp = (st_ - 1) % 2
            S.add("dve", lambda e, lo=lo, sh=sh, pp=pp, prev=prev: e.tensor_tensor(
                out=sab[:, pp, lo:HB + TW], in0=prev(lo, HB + TW), in1=prev(lo - sh, HB + TW - sh), op=ALU.add),
                reads=prev_keys, writes=[("sa", pp)])
            prev, prev_keys = (lambda lo, hi, pp=pp: sab[:, pp, lo:hi]), [("sa", pp)]
        S.add("dve", lambda e, prev=prev: e.scalar_tensor_tensor(
            out=dT[:, ds, :], in0=prev(HB, HB + TW), scalar=1.0 / w, in1=cvb[:, us, HB:HB + TW],
            op0=ALU.mult, op1=ALU.subtract),
            reads=prev_keys + [("cv", us)], writes=[("d", ds)])
        if first_tile:
            wi = POOL_WINDOWS.index(w)
            S.add("dve", lambda e, prev=prev: e.tensor_tensor(out=fix[:, 0:15], in0=prev(HB, HB + 15),
                                                              in1=invc[:, wi, 0:15], op=ALU.mult),
                  reads=prev_keys + [("invcw", wi)], writes=[("fix",)])
            S.add("dve", lambda e: e.tensor_tensor(out=dT[:, ds, 0:15], in0=fix[:, 0:15], in1=cvb[:, us, HB:HB + 15],
                                                   op=ALU.subtract),
                  reads=[("fix",), ("cv", us)], writes=[("d", ds)])
        return ds, zs

    def stage_B_grp(i, g, j, dss, zss):
        bl = i // 2
        for oc in range(2):
            b = bank()
            for ic in range(2):
                S.add("pe", lambda e, oc=oc, ic=ic, b=b: e.matmul(
                    ps[:, b, :], lhsT=wgrp[:, g, ic, oc * 128:(oc + 1) * 128], rhs=dT[:, dss[ic], :],
                    start=(ic == 0), stop=(ic == 1)),
                    reads=[("wgrp",), ("d", dss[ic])], writes=[("ps", b)])
            c = 2 * g + oc
            S.add("dve", lambda e, b=b, c=c, oc=oc: e.scalar_tensor_tensor(
                out=mT[:, c, tsl(j)], in0=ps[:, b, :], scalar=pcol(PC_SCALE + bl * KC + c), in1=szb[:, zss[oc], :],
                op0=ALU.mult, op1=ALU.mult),
                reads=[("ps", b), ("sz", zss[oc]), ("par",)], writes=[("m", c, j)])

    def stage_out(i, j, hooks=None):
        korder = chunk_order(i)
        for o in range(KC):
            b = bank()
            for n_, k in enumerate(korder):
                S.add("pe", lambda e, o=o, k=k, b=b, n_=n_: e.matmul(
                    ps[:, b, :], lhsT=wout[:, k, o * 128:(o + 1) * 128], rhs=mT[:, k, tsl(j)],
                    start=(n_ == 0), stop=(n_ == KC - 1)),
                    reads=[("wout", k // 4), ("m", k, j)], writes=[("ps", b)])
            S.add("dve", lambda e, o=o, b=b: e.tensor_tensor(out=hT[:, o, tsl(j)], in0=hT[:, o, tsl(j)],
                                                            in1=ps[:, b, :], op=ALU.add),
                  reads=[("h", o, j), ("ps", b)], writes=[("h", o, j)])
            if hooks and o in hooks:
                hooks[o]()

    def stage_ple(i, j, hooks=None):
        for o in range(KC):
            bg, bp = bank(), bank()
            for k2 in range(2):
                S.add("pe", lambda e, o=o, k2=k2, bp=bp: e.matmul(ps[:, bp, :],
                                                                 lhsT=wproj[:, k2, o * 128:(o + 1) * 128],
                                                                 rhs=pT[:, j, k2, :], start=(k2 == 0),
                                                                 stop=(k2 == 1)),
                      reads=[("wproj",), ("pT", j)], writes=[("ps", bp)])
            for k in range(KC):
                S.add("pe", lambda e, o=o, k=k, bg=bg: e.matmul(ps[:, bg, :], lhsT=wgate[:, k, o * 128:(o + 1) * 128],
                                                               rhs=hn[:, k, tsl(j)], start=(k == 0),
                                                               stop=(k == KC - 1)),
                      reads=[("wgate", o // 4), ("hn", k, j)], writes=[("ps", bg)])
            ts_ = nxt("t", 2)
            ms = nxt("tmp", 2)
            S.add("act", lambda e, bg=bg, ts_=ts_: e.activation(out=tb[:, ts_, :], in_=ps[:, bg, :], func=AF.Tanh,
                                                               scale=0.5),
                  reads=[("ps", bg)], writes=[("t", ts_)])
            S.add("dve", lambda e, bp=bp, ts_=ts_, ms=ms: e.scalar_tensor_tensor(
                out=tmpb[:, ms, :], in0=tb[:, ts_, :], scalar=1.0, in1=ps[:, bp, :], op0=ALU.add, op1=ALU.mult),
                reads=[("t", ts_), ("ps", bp)], writes=[("tmp", ms)])
            S.add("dve", lambda e, o=o, ms=ms: e.scalar_tensor_tensor(
                out=hT[:, o, tsl(j)], in0=tmpb[:, ms, :], scalar=0.5, in1=hT[:, o, tsl(j)],
                op0=ALU.mult, op1=ALU.add),
                reads=[("tmp", ms), ("h", o, j)], writes=[("h", o, j)])
            if hooks and o in hooks:
                hooks[o]()

    fin = {"o": 0}

    def stage_T(sup, j, blocks=(0, 1, 2, 3)):
        for n in blocks:
            b = bank2()
            pb = pair(b)
            c0 = j * TW + n * 128
            for k in range(KC):
                S.add("pe", lambda e, k=k, pb=pb, c0=c0: e.transpose(
                    out=pb[:, k * 128:(k + 1) * 128], in_=hT[:, k, c0:c0 + 128], identity=ident[:]),
                    reads=[("h", k, j), ("ident",)], writes=[("ps", b), ("ps", b + 1)])
            os_ = fin["o"] % 4
            cc = fin["o"] % 8
            fin["o"] += 1
            S.add("act", lambda e, pb=pb, cc=cc: e.activation(out=junk[:], in_=pb, func=AF.Square,
                                                              accum_out=ssq[:, cc:cc + 1]),
                  reads=[("ps", b), ("ps", b + 1)], writes=[("junk",), ("ssq", cc)])
            S.add("pool", lambda e, cc=cc: e.tensor_tensor(out=rt[:, cc:cc + 1], in0=ssq[:, cc:cc + 1],
                                                           in1=cst[:, 0:1], op=ALU.add),
                  reads=[("ssq", cc), ("cst",)], writes=[("rt", cc)])
            S.add("pool", lambda e, cc=cc: e.tensor_tensor(out=rt[:, cc:cc + 1], in0=rt[:, cc:cc + 1],
                                                           in1=cst[:, 1:2], op=ALU.pow),
                  reads=[("rt", cc), ("cst",)], writes=[("rt", cc)])
            S.add("dve", lambda e, pb=pb, os_=os_, cc=cc: e.scalar_tensor_tensor(
                out=ost[:, os_, :], in0=pb, scalar=rt[:, cc:cc + 1], in1=gbc[:], op0=ALU.mult, op1=ALU.mult),
                reads=[("ps", b), ("ps", b + 1), ("rt", cc), ("gbc",)], writes=ost_keys(os_))
            r0 = sup * SUP + j * TW + n * 128
            S.add("sp", lambda e, os_=os_, r0=r0: e.dma_start(out=y_d[r0:r0 + 128, :], in_=ost[:, os_, :]),
                  reads=ost_keys(os_), writes=[("y", r0)], dma=("ost", os_))

    for blk in range(4):
        load_x_block(0, blk)
    load_p(0, 0)
    ensure_pieces(2)
    pst_ = {"pidx": 0}

    def piece_done():
        pst_["pidx"] += 1
        ensure_pieces(pst_["pidx"] + 2)

    def mixer_phase(i, sup, n1_tile1, smalls):
        order = chunk_order(i)
        first = (sup == 0)

        def small():
            if smalls:
                smalls.pop(0)()
        if i % 2 == 0:
            c0, c1 = order[0], order[1]
            s0, s1 = pst_["pidx"] % 3, (pst_["pidx"] + 1) % 3
            small()
            n1_tile1.a()
            stage_A(i, c0, 0, s0, kouter=True)
            n1_tile1.b()
            small()
            stage_A(i, c1, 0, s1)
            stage_A(i, c0, 1, s0, kouter=True)
            piece_done()
            stage_A(i, c1, 1, s1)
            piece_done()
            for c in order[2:]:
                slot = pst_["pidx"] % 3
                small()
                stage_A(i, c, 0, slot)
                stage_A(i, c, 1, slot)
                piece_done()
        else:
            res = {}
            pend = []

            def uz(c, j, slot, kouter=False):
                res[(c, j)] = stage_B_uz(i, c, j, slot, first and j == 0, kouter)
                while len(pend) > 0 and pend[0][2] < uz.n:
                    g, jj, _ = pend.pop(0)
                    emit_grp(g, jj)
                uz.n += 1
                g = c // 2
                if (2 * g, j) in res and (2 * g + 1, j) in res:
                    pend.append((g, j, uz.n))
            uz.n = 0

            def emit_grp(g, j):
                a, b_ = res[(2 * g, j)], res[(2 * g + 1, j)]
                stage_B_grp(i, g, j, (a[0], b_[0]), (a[1], b_[1]))
            c0, c1 = order[0], order[1]
            s0, s1 = pst_["pidx"] % 3, (pst_["pidx"] + 1) % 3
            small()
            n1_tile1.a()
            uz(c0, 0, s0, kouter=True)
            n1_tile1.b()
            small()
            uz(c1, 0, s1)
            uz(c0, 1, s0, kouter=True)
            piece_done()
            uz(c1, 1, s1)
            piece_done()
            for c in order[2:]:
                slot = pst_["pidx"] % 3
                small()
                uz(c, 0, slot)
                uz(c, 1, slot)
                piece_done()
            while pend:
                g, jj, _ = pend.pop(0)
                emit_grp(g, jj)
        while smalls:
            smalls.pop(0)()

    for sup in range(nsup):
        if sup == 0:
            stage_X(sup, 0)
            Norm(0, PC_NMIX + 0 * KC).both()
            stage_X(sup, 1)
        if sup == 0:
            dump("x")
        for i in range(DEPTH):
            smalls = small_loads(i, sup)
            if sup == 0 and i == 0:
                smalls.append(lambda: loads_wout(0))
            for j in range(NT):
                stage_P(i, j)
            mixer_phase(i, sup, Norm(1, PC_NMIX + i * KC), smalls)
            if sup == 0:
                dump("m%d" % i, "m")
            n2_0 = Norm(0, PC_NPLE + i * KC)
            n2_1 = Norm(1, PC_NPLE + i * KC)
            last = (i == DEPTH - 1)
            more = sup + 1 < nsup
            stage_out(i, 0)
            load_p_next(i, sup)
            stage_out(i, 1, hooks={0: n2_0.a, 3: n2_0.b})
            load_wout_next(i, sup)
            if last and more:
                for blk in range(4):
                    load_x_block(sup + 1, blk)
            if sup == 0:
                dump("mix%d" % i)
            stage_ple(i, 0, hooks={0: n2_1.a, 2: n2_1.b})
            if not last:
                nx0 = Norm(0, PC_NMIX + (i + 1) * KC)
                stage_ple(i, 1, hooks={0: nx0.a, 2: nx0.b})
            else:
                hk = {0: (lambda: stage_T(sup, 0, (0,))), 1: (lambda: stage_T(sup, 0, (1,))),
                      2: (lambda: stage_T(sup, 0, (2,))), 3: (lambda: stage_T(sup, 0, (3,)))}
                if more:
                    hk[4] = lambda: stage_X(sup + 1, 0, (0, 1))
                    hk[5] = lambda: stage_X(sup + 1, 0, (2, 3))
                stage_ple(i, 1, hooks=hk)
            if sup == 0:
                dump("L%d" % i)
        if sup + 1 < nsup:
            n10 = Norm(0, PC_NMIX + 0 * KC)
            n10.a()
            stage_T(sup, 1)
            n10.b()
            stage_X(sup + 1, 1)
        else:
            stage_T(sup, 1)

    S.finalize()
    sems = {}
    for eng in Sched.ENGS:
        sems[("eng", eng)] = es.enter_context(nc.semaphore("c_" + eng))
    for n_, k in enumerate(S.dma_keys):
        sems[("dma", k)] = es.enter_context(nc.semaphore("d%d" % n_))

    with nc.Block() as block:
        @block.tensor
        def _(e):
            S.emit_engine("pe", e, sems)

        @block.scalar
        def _(e):
            S.emit_engine("act", e, sems)

        @block.vector
        def _(e):
            S.emit_engine("dve", e, sems)

        @block.gpsimd
        def _(e):
            S.emit_engine("pool", e, sems)

        @block.sync
        def _(e):
            waited = S.emit_engine("sp", e, sems)
            for k, v in S.final_dma.items():
                if k[0] in ("ost", "dbg") and waited.get(("dma", k), 0) < v:
                    e.wait_ge(sems[("dma", k)], v)
    es.close()
    return nc


def feature_major(v):
    v = np.asarray(v, dtype=np.float32).reshape(-1, KC, 128)
    return np.ascontiguousarray(v.transpose(2, 0, 1).reshape(128, -1))


def make_params(norm_mix, ple_norm, final_norm, a_w_conv, b_scale):
    cols = [feature_major(norm_mix), feature_major(ple_norm), feature_major(final_norm),
            feature_major(a_w_conv), feature_major(b_scale)]
    par = np.ascontiguousarray(np.concatenate(cols, axis=1))
    assert par.shape == (128, NPAR), par.shape
    return par


_NC_CACHE = {}


def kernel(x, p, norm_mix, a_w_in, a_w_conv, a_w_out, b_w_in, b_w_grp, b_scale, b_w_out,
           ple_norm, ple_w_gate, ple_w_proj, final_norm):
    n = 8
    if "nc" not in _NC_CACHE:
        _NC_CACHE["nc"] = build_program()
    nc = _NC_CACHE["nc"]
    par = make_params(norm_mix, ple_norm, final_norm, a_w_conv, b_scale)
    shared = {
        "params": par,
        "gfin": np.ascontiguousarray(np.asarray(final_norm, dtype=np.float32).reshape(1, D)),
        "a_w_in": np.ascontiguousarray(a_w_in, dtype=np.float32),
        "a_w_out": np.ascontiguousarray(a_w_out, dtype=np.float32),
        "b_w_in": np.ascontiguousarray(b_w_in, dtype=np.float32),
        "b_w_grp": np.ascontiguousarray(b_w_grp, dtype=np.float32),
        "b_w_out": np.ascontiguousarray(b_w_out, dtype=np.float32),
        "ple_w_gate": np.ascontiguousarray(ple_w_gate, dtype=np.float32),
        "ple_w_proj": np.ascontiguousarray(ple_w_proj, dtype=np.float32),
    }
    in_maps = []
    for b in range(n):
        m = dict(shared)
        m["x"] = np.ascontiguousarray(x[b], dtype=np.float32)
        m["p"] = np.ascontiguousarray(p[:, b], dtype=np.float32)
        in_maps.append(m)
    res = run_bass_kernel_spmd(nc, in_maps, core_ids=list(range(n)))
    return np.stack([np.asarray(r["y"], dtype=np.float32) for r in res.results], axis=0)
```

```python
import numpy as np
from contextlib import ExitStack

import concourse.bass as bass
import concourse.mybir as mybir
from concourse.bass_utils import run_bass_kernel_spmd

F32 = mybir.dt.float32
BF16 = mybir.dt.bfloat16
AF = mybir.ActivationFunctionType
ALU = mybir.AluOpType

D = 1024
SEQ = 4096
DEPTH = 4
PLE = 256
EPS = 1e-6
KC = 8
TW = 512
NT = 2
SUP = TW * NT
POOL_WINDOWS = (2, 4, 8, 16)
HB = 16

PC_NMIX = 0
PC_NPLE = 32
PC_NFIN = 64
PC_CONV = 72
PC_SCALE = 120
NPAR = 136
NG32 = 72


class Sched:
    ENGS = ("pe", "act", "dve", "pool", "sp")

    def __init__(self):
        self.ops = []
        self.last_w = {}
        self.readers = {}
        self.eng_ops = {e: [] for e in self.ENGS}
        self.dma_keys = []

    def add(self, eng, fn, reads=(), writes=(), dma=None, ninst=1):
        oid = len(self.ops)
        deps = set()
        for r in reads:
            w = self.last_w.get(r)
            if w is not None:
                deps.add(w)
        for w_ in writes:
            w = self.last_w.get(w_)
            if w is not None:
                deps.add(w)
            for rd in self.readers.get(w_, ()):
                deps.add(rd)
        deps.discard(oid)
        for r in reads:
            self.readers.setdefault(r, []).append(oid)
        for w_ in writes:
            self.last_w[w_] = oid
            self.readers[w_] = []
        op = dict(id=oid, eng=eng, fn=fn, deps=deps, dma=dma, ninst=ninst,
                  seq=len(self.eng_ops[eng]), signal=False, tok=None)
        if dma is not None and dma not in self.dma_keys:
            self.dma_keys.append(dma)
        self.ops.append(op)
        self.eng_ops[eng].append(op)
        return oid

    def _needed(self, op, d):
        if d["dma"] is not None:
            return True
        if d["eng"] != op["eng"]:
            return True
        if op["eng"] == "pe":
            return False
        return True

    def finalize(self):
        for op in self.ops:
            for did in op["deps"]:
                d = self.ops[did]
                if d["dma"] is None and self._needed(op, d):
                    d["signal"] = True
        cnt = {e: 0 for e in self.ENGS}
        dcnt = {k: 0 for k in self.dma_keys}
        for op in self.ops:
            if op["dma"] is not None:
                dcnt[op["dma"]] += 16 * op["ninst"]
                op["tok"] = (("dma", op["dma"]), dcnt[op["dma"]])
            elif op["signal"]:
                cnt[op["eng"]] += 1
                op["tok"] = (("eng", op["eng"]), cnt[op["eng"]])
        self.final_dma = dict(dcnt)
        self.final_cnt = dict(cnt)

    def emit_engine(self, eng, e, sems):
        waited = {}
        attach = eng != "pe"
        for op in self.eng_ops[eng]:
            need = {}
            for did in op["deps"]:
                d = self.ops[did]
                if not self._needed(op, d):
                    continue
                key, val = d["tok"]
                if waited.get(key, 0) >= val:
                    continue
                if need.get(key, 0) < val:
                    need[key] = val
            items = list(need.items())
            sep = items[:-1] if attach else items
            for key, val in sep:
                e.wait_ge(sems[key], val)
                waited[key] = val
            insts = op["fn"](e)
            if not isinstance(insts, (list, tuple)):
                insts = [insts]
            if attach and items:
                key, val = items[-1]
                insts[0]._wait_ge(sems[key], val)
                waited[key] = val
            if op["dma"] is not None:
                assert len(insts) == op["ninst"], (len(insts), op["ninst"])
                for ins in insts:
                    ins.then_inc(sems[("dma", op["dma"])], 16)
            elif op["signal"]:
                insts[-1].then_inc(sems[("eng", eng)], 1)
        return waited


def chunk_order(i):
    return list(range(KC)) if i % 2 == 0 else [6, 7, 4, 5, 2, 3, 0, 1]


def build_program(nsup=SEQ // SUP, ntok=SEQ, debug=()):
    nc = bass.Bass("TRN2", target_bir_lowering=False)
    S = Sched()
    if debug:
        dbg_d = nc.dram_tensor("dbg", [len(debug), 128, KC * SUP], F32, kind="ExternalOutput").ap()

    x_d = nc.dram_tensor("x", [ntok, D], F32, kind="ExternalInput").ap()
    p_d = nc.dram_tensor("p", [DEPTH, ntok, PLE], F32, kind="ExternalInput").ap()
    par_d = nc.dram_tensor("params", [128, NPAR], F32, kind="ExternalInput").ap()
    awin_d = nc.dram_tensor("a_w_in", [2, D, 4 * D], F32, kind="ExternalInput").ap()
    awout_d = nc.dram_tensor("a_w_out", [2, D, D], F32, kind="ExternalInput").ap()
    bwin_d = nc.dram_tensor("b_w_in", [2, D, 2 * D], F32, kind="ExternalInput").ap()
    bwgrp_d = nc.dram_tensor("b_w_grp", [2, 4, 256, 256], F32, kind="ExternalInput").ap()
    bwout_d = nc.dram_tensor("b_w_out", [2, D, D], F32, kind="ExternalInput").ap()
    wgate_d = nc.dram_tensor("ple_w_gate", [DEPTH, D, D], F32, kind="ExternalInput").ap()
    wproj_d = nc.dram_tensor("ple_w_proj", [DEPTH, PLE, D], F32, kind="ExternalInput").ap()
    gfin_d = nc.dram_tensor("gfin", [1, D], F32, kind="ExternalInput").ap()
    y_d = nc.dram_tensor("y", [ntok, D], F32, kind="ExternalOutput").ap()

    es = ExitStack()

    def sb(name, shape, dt):
        return es.enter_context(nc.sbuf_tensor(name, shape, dt))

    NSZ = 8
    ND = 8
    hT = sb("hT", [128, KC, SUP], F32)
    hn = sb("hn", [128, KC, SUP], BF16)
    mT = sb("mT", [128, KC, SUP], BF16)
    win = sb("win", [128, 3, KC, 4, 128], BF16)
    wout = sb("wout", [128, KC, D], BF16)
    wgate = sb("wgate", [128, KC, D], BF16)
    wproj = sb("wproj", [128, 2, D], BF16)
    wgrp = sb("wgrp", [128, 4, 2, 256], BF16)
    pT = sb("pT", [128, NT, 2, TW], BF16)
    pst = sb("pst", [128, SUP // 128, PLE], BF16)
    sq = sb("sq", [128, KC, TW], BF16)
    rr = sb("rr", [128, 2, TW], F32)
    szb = sb("szb", [128, NSZ, TW], F32)
    cgb = sb("cgb", [128, 2, TW], F32)
    cvb = sb("cvb", [128, 2, HB + TW], F32)
    yb = sb("yb", [128, 2, TW], F32)
    tb = sb("tb", [128, 2, TW], F32)
    tmpb = sb("tmpb", [128, 2, TW], F32)
    sab = sb("sab", [128, 2, HB + TW], F32)
    dT = sb("dT", [128, ND, TW], BF16)
    fix = sb("fix", [128, 16], F32)
    cbias = sb("cbias", [128, 1], F32)
    par = sb("par", [128, NPAR], F32)
    ident = sb("ident", [128, 128], F32)
    identb = sb("identb", [128, 128], BF16)
    ones = sb("ones", [128, 128], BF16)
    invc = sb("invc", [128, 5, 16], F32)
    halo_a = sb("halo_a", [128, 2, KC, 2], F32)
    halo_b = sb("halo_b", [128, 2, KC, 16], F32)
    gbc = sb("gbc", [128, D], F32)
    junk = sb("junk", [128, D], BF16)
    ssq = sb("ssq", [128, 8], F32)
    rt = sb("rt", [128, 8], F32)
    cst = sb("cst", [128, 2], F32)
    ps = es.enter_context(nc.psum_tensor("ps", [128, 8, TW], F32))

    xst = mT[:].bitcast(F32)
    ost = szb[:, :, :].rearrange("p (s a) t -> p s (a t)", a=2)

    def xst_slot(s):
        return xst[:, 2 * s:2 * s + 2, :].rearrange("p a b -> p (a b)")

    def xst_keys(s):
        return [("m", 2 * s + a, j) for a in (0, 1) for j in range(NT)]

    def ost_keys(s):
        return [("sz", 2 * s), ("sz", 2 * s + 1)]

    def dump(name, kind="h"):
        if name not in debug:
            return
        idx = list(debug).index(name)
        src, key = {"h": (hT, "h"), "hn": (hn, "hn"), "m": (mT, "m")}[kind]
        dst = dbg_d[idx] if kind == "h" else dbg_d[idx].bitcast(BF16)[:, 0:KC * SUP]
        S.add("sp", lambda e: e.dma_start(out=dst, in_=src[:].rearrange("p k t -> p (k t)")),
              reads=[(key, k, j) for k in range(KC) for j in range(NT)], writes=[("dbg", idx)], dma=("dbg", idx))

    ring = {"n": 0}

    def bank():
        b = ring["n"] % 8
        ring["n"] += 1
        return b

    def bank2():
        if ring["n"] % 2:
            ring["n"] += 1
        b = ring["n"] % 8
        ring["n"] += 2
        return b

    def pair(b):
        return ps[:, b:b + 2, :].rearrange("p a b -> p (a b)")

    def tsl(j):
        return slice(j * TW, (j + 1) * TW)

    def pcol(c):
        return par[:, c:c + 1]

    S.add("sp", lambda e: e.dma_start(out=par[:], in_=par_d[:, :]), writes=[("par",)], dma="par")
    S.add("pool", lambda e: e.memset(ident[:], 0.0), writes=[("ident",)])
    S.add("pool", lambda e: e.affine_select(out=ident[:], in_=ident[:], compare_op=ALU.not_equal, fill=1.0,
                                            base=0, pattern=[[-1, 128]], channel_multiplier=1),
          reads=[("ident",)], writes=[("ident",)])
    S.add("pool", lambda e: e.tensor_copy(out=identb[:], in_=ident[:]), reads=[("ident",)], writes=[("identb",)])
    S.add("pool", lambda e: e.memset(ones[:], 1.0), writes=[("ones",)])
    S.add("pool", lambda e: e.memset(halo_a[:], 0.0), writes=[("halo_a", a, c) for a in range(2) for c in range(KC)])
    S.add("pool", lambda e: e.memset(halo_b[:], 0.0), writes=[("halo_b", a, c) for a in range(2) for c in range(KC)])

    def mk_inv(t):
        return lambda e: e.memset(invc[:, 4, t:t + 1], 1.0 / (t + 1))
    for t in range(16):
        S.add("pool", mk_inv(t), writes=[("invc",)])

    def mk_invw(wi, w):
        return lambda e: e.tensor_scalar(out=invc[:, wi, :], in0=invc[:, 4, :], scalar1=1.0 / w, scalar2=None,
                                         op0=ALU.max)
    for wi, w in enumerate(POOL_WINDOWS):
        S.add("dve", mk_invw(wi, w), reads=[("invc",)], writes=[("invcw", wi)])
    S.add("pool", lambda e: e.memset(cbias[:], float(D * EPS)), writes=[("cbias",)])
    S.add("sp", lambda e: e.dma_start(out=gbc[:], in_=gfin_d[0:1, :].to_broadcast([128, D])),
          writes=[("gbc",)], dma="gbc")
    S.add("dve", lambda e: e.tensor_scalar(out=gbc[:], in0=gbc[:], scalar1=32.0, scalar2=None, op0=ALU.mult),
          reads=[("gbc",)], writes=[("gbc",)])
    S.add("pool", lambda e: e.memset(cst[:, 0:1], float(D * EPS)), writes=[("cst",)])
    S.add("pool", lambda e: e.memset(cst[:, 1:2], -0.5), writes=[("cst",)])
    S.add("dve", lambda e: e.tensor_scalar(out=par[:, 0:NG32], in0=par[:, 0:NG32], scalar1=32.0, scalar2=None,
                                           op0=ALU.mult), reads=[("par",)], writes=[("par",)])

    pieces = []
    for s_ in range(nsup):
        for i in range(DEPTH):
            for c in chunk_order(i):
                pieces.append((i, c))
    st = {"next_piece": 0}

    def load_piece(n):
        i, c = pieces[n]
        slot = n % 3
        if i % 2 == 0:
            src = awin_d[i // 2].rearrange("(k p) (g c e) -> p k g c e", p=128, g=4, c=KC)
            ng = 4
        else:
            src = bwin_d[i // 2].rearrange("(k p) (g c e) -> p k g c e", p=128, g=2, c=KC)
            ng = 2

        def fn(e):
            return [e.dma_start(out=win[:, slot, :, g, :], in_=src[:, :, g, c, :]) for g in range(ng)]
        S.add("pool", fn, writes=[("win", slot)], dma=("win", slot), ninst=ng)

    def ensure_pieces(upto):
        while st["next_piece"] <= min(upto, len(pieces) - 1):
            load_piece(st["next_piece"])
            st["next_piece"] += 1

    def load_p(i, sup):
        src = p_d[i, sup * SUP:(sup + 1) * SUP, :].rearrange("(n p) d -> p n d", p=128)
        S.add("pool", lambda e: e.dma_start(out=pst[:, :, :], in_=src), writes=[("pst",)], dma=("pst",))

    def loads_wout(i):
        wo_src = (awout_d[i // 2] if i % 2 == 0 else bwout_d[i // 2]).rearrange("(k p) n -> p k n", p=128)
        if i % 2 == 1:
            def fn(e):
                return [e.dma_start(out=wgrp[:, g, :, :],
                                    in_=bwgrp_d[i // 2, g].rearrange("(ic p) n -> p ic n", p=128))
                        for g in range(4)]
            S.add("pool", fn, writes=[("wgrp",)], dma=("wgrp",), ninst=4)
        for h in range(2):
            ks = slice(4 * h, 4 * h + 4)
            S.add("pool", lambda e, ks=ks: e.dma_start(out=wout[:, ks, :], in_=wo_src[:, ks, :]),
                  writes=[("wout", h)], dma=("wout", h))

    def small_loads(i, sup):
        out = []
        wg_src = wgate_d[i].rearrange("(k p) n -> p k n", p=128)
        wp_src = wproj_d[i].rearrange("(k p) n -> p k n", p=128)
        for h in range(2):
            cs = slice(512 * h, 512 * h + 512)
            out.append(lambda cs=cs, h=h: S.add(
                "pool", lambda e: e.dma_start(out=wgate[:, :, cs], in_=wg_src[:, :, cs]),
                writes=[("wgate", h)], dma=("wgate", h)))
        out.append(lambda: S.add("pool", lambda e: e.dma_start(out=wproj[:], in_=wp_src),
                                 writes=[("wproj",)], dma=("wproj",)))
        return out

    def load_p_next(i, sup):
        ni, nsup_ = (i + 1, sup) if i + 1 < DEPTH else (0, sup + 1)
        if nsup_ < nsup:
            load_p(ni, nsup_)

    def load_wout_next(i, sup):
        ni, nsup_ = (i + 1, sup) if i + 1 < DEPTH else (0, sup + 1)
        if nsup_ < nsup:
            loads_wout(ni)

    def load_x_block(sup, blk):
        slot = blk % 4
        r0 = sup * SUP + blk * 128
        S.add("sp", lambda e: e.dma_start(out=xst_slot(slot), in_=x_d[r0:r0 + 128, :]),
              writes=xst_keys(slot), dma=("xst", slot))

    def stage_X(sup, j, blocks=(0, 1, 2, 3)):
        for n in blocks:
            blk = j * 4 + n
            slot = blk % 4
            b = bank2()
            pv = pair(b).rearrange("p (k t) -> p k t", t=128)
            for k in range(KC):
                S.add("pe", lambda e, k=k, pv=pv, slot=slot: e.transpose(
                    out=pv[:, k, :], in_=xst_slot(slot)[:, k * 128:(k + 1) * 128], identity=ident[:]),
                    reads=xst_keys(slot) + [("ident",)], writes=[("ps", b), ("ps", b + 1)])
            c0 = j * TW + n * 128
            S.add("act", lambda e, pv=pv, c0=c0: e.activation(out=hT[:, :, c0:c0 + 128], in_=pv, func=AF.Copy),
                  reads=[("ps", b), ("ps", b + 1)], writes=[("h", k, j) for k in range(KC)])
            if blk + 4 < SUP // 128:
                load_x_block(sup, blk + 4)

    nst = {"r": 0}

    def norm_sq(j):
        for k in range(KC):
            S.add("act", lambda e, k=k: e.activation(out=sq[:, k, :], in_=hT[:, k, tsl(j)], func=AF.Square),
                  reads=[("h", k, j)], writes=[("sq", k)])

    def norm_stats(j):
        b = bank()
        for k in range(KC):
            S.add("pe", lambda e, k=k: e.matmul(ps[:, b, :], lhsT=ones[:], rhs=sq[:, k, :],
                                                start=(k == 0), stop=(k == KC - 1)),
                  reads=[("sq", k), ("ones",)], writes=[("ps", b)])
        rs = nst["r"] % 2
        nst["r"] += 1
        S.add("act", lambda e: e.activation(out=rr[:, rs, :], in_=ps[:, b, :], func=AF.Ln, bias=cbias[:, 0:1],
                                            scale=1.0),
              reads=[("ps", b), ("cbias",)], writes=[("rr", rs)])
        S.add("act", lambda e: e.activation(out=rr[:, rs, :], in_=rr[:, rs, :], func=AF.Exp, scale=-0.5),
              reads=[("rr", rs)], writes=[("rr", rs)])
        return rs

    def norm_apply(j, rs, gcol, dst, dst_keys):
        for k in range(KC):
            S.add("dve", lambda e, k=k: e.scalar_tensor_tensor(
                out=dst(k), in0=hT[:, k, tsl(j)], scalar=pcol(gcol + k), in1=rr[:, rs, :],
                op0=ALU.mult, op1=ALU.mult),
                reads=[("h", k, j), ("rr", rs), ("par",)], writes=dst_keys(k))

    class Norm:
        def __init__(self, j, gcol):
            self.j, self.gcol = j, gcol

        def a(self):
            norm_sq(self.j)

        def b1(self):
            self.rs = norm_stats(self.j)

        def b2(self):
            j, rs = self.j, self.rs
            norm_apply(j, rs, self.gcol, (lambda k: hn[:, k, tsl(j)]), (lambda k: [("hn", k, j)]))

        def b(self):
            self.b1()
            self.b2()

        def both(self):
            self.a()
            self.b()

    cnt = {"sz": 0, "cg": 0, "cv": 0, "y": 0, "t": 0, "tmp": 0, "d": 0}

    def nxt(name, n):
        v = cnt[name] % n
        cnt[name] += 1
        return v

    def stage_P(i, j):
        b = bank()
        pv = ps[:, b, :].bitcast(BF16).rearrange("p (k t) -> p k t", k=2)
        for k2 in range(2):
            for n in range(4):
                S.add("pe", lambda e, k2=k2, n=n: e.transpose(
                    out=pv[:, k2, n * 128:(n + 1) * 128], in_=pst[:, j * 4 + n, k2 * 128:(k2 + 1) * 128],
                    identity=identb[:]),
                    reads=[("pst",), ("identb",)], writes=[("ps", b)])
        S.add("act", lambda e: e.activation(out=pT[:, j, :, :], in_=pv, func=AF.Copy),
              reads=[("ps", b)], writes=[("pT", j)])

    def mm_group(banks, lhs_of, j, kouter):
        ng = len(banks)
        order = [(g, k) for k in range(KC) for g in range(ng)] if kouter else \
                [(g, k) for g in range(ng) for k in range(KC)]
        for g, k in order:
            lhs, rd = lhs_of(g, k)
            S.add("pe", lambda e, g=g, k=k, lhs=lhs: e.matmul(ps[:, banks[g], :], lhsT=lhs, rhs=hn[:, k, tsl(j)],
                                                             start=(k == 0), stop=(k == KC - 1)),
                  reads=rd + [("hn", k, j)], writes=[("ps", banks[g])])

    def stage_A(i, c, j, slot, kouter=False):
        al = i // 2
        bs = [bank() for _ in range(4)]
        gmap = (1, 2, 0, 3)
        mm_group(bs, lambda g, k: (win[:, slot, k, gmap[g], :], [("win", slot)]), j, kouter)
        b_cg, b_v, b_bg, b_z = bs
        zs = nxt("sz", NSZ)
        gs = nxt("cg", 2)
        vs = nxt("cv", 2)
        ys = nxt("y", 2)
        w0 = PC_CONV + (al * 3 + 0) * KC + c
        w1 = PC_CONV + (al * 3 + 1) * KC + c
        w2 = PC_CONV + (al * 3 + 2) * KC + c
        S.add("act", lambda e: e.activation(out=cgb[:, gs, :], in_=ps[:, b_cg, :], func=AF.Copy),
              reads=[("ps", b_cg)], writes=[("cg", gs)])
        S.add("act", lambda e: e.activation(out=cvb[:, vs, HB - 2:HB], in_=halo_a[:, al, c, :], func=AF.Copy),
              reads=[("halo_a", al, c)], writes=[("cvh", vs)])
        S.add("dve", lambda e: e.tensor_tensor(out=cvb[:, vs, HB:HB + TW], in0=cgb[:, gs, :], in1=ps[:, b_v, :],
                                               op=ALU.mult),
              reads=[("cg", gs), ("ps", b_v)], writes=[("cv", vs)])
        S.add("act", lambda e: e.activation(out=halo_a[:, al, c, :], in_=cvb[:, vs, HB + TW - 2:HB + TW],
                                            func=AF.Copy),
              reads=[("cv", vs)], writes=[("halo_a", al, c)])
        S.add("act", lambda e: e.activation(out=yb[:, ys, :], in_=cvb[:, vs, HB - 2:HB - 2 + TW], func=AF.Identity,
                                            scale=pcol(w0)),
              reads=[("cv", vs), ("cvh", vs), ("par",)], writes=[("y", ys)])
        S.add("dve", lambda e: e.scalar_tensor_tensor(out=yb[:, ys, :], in0=cvb[:, vs, HB - 1:HB - 1 + TW],
                                                      scalar=pcol(w1), in1=yb[:, ys, :], op0=ALU.mult, op1=ALU.add),
              reads=[("cv", vs), ("cvh", vs), ("y", ys), ("par",)], writes=[("y", ys)])
        S.add("dve", lambda e: e.scalar_tensor_tensor(out=yb[:, ys, :], in0=cvb[:, vs, HB:HB + TW],
                                                      scalar=pcol(w2), in1=yb[:, ys, :], op0=ALU.mult, op1=ALU.add),
              reads=[("cv", vs), ("y", ys), ("par",)], writes=[("y", ys)])
        S.add("act", lambda e: e.activation(out=szb[:, zs, :], in_=ps[:, b_z, :], func=AF.Silu),
              reads=[("ps", b_z)], writes=[("sz", zs)])
        S.add("dve", lambda e: e.tensor_tensor(out=szb[:, zs, :], in0=szb[:, zs, :], in1=ps[:, b_bg, :],
                                               op=ALU.mult),
              reads=[("sz", zs), ("ps", b_bg)], writes=[("sz", zs)])
        S.add("dve", lambda e: e.tensor_tensor(out=mT[:, c, tsl(j)], in0=szb[:, zs, :], in1=yb[:, ys, :],
                                               op=ALU.mult),
              reads=[("sz", zs), ("y", ys)], writes=[("m", c, j)])

    def stage_B_uz(i, c, j, slot, first_tile, kouter=False):
        bl = i // 2
        w = POOL_WINDOWS[c // 2]
        bu, bz = bank(), bank()
        mm_group([bu, bz], lambda g, k: (win[:, slot, k, g, :], [("win", slot)]), j, kouter)
        zs = nxt("sz", NSZ)
        us = nxt("cv", 2)
        ds = nxt("d", ND)
        S.add("act", lambda e: e.activation(out=cvb[:, us, 1:HB], in_=halo_b[:, bl, c, 0:15], func=AF.Copy),
              reads=[("halo_b", bl, c)], writes=[("cvh", us)])
        S.add("act", lambda e: e.activation(out=cvb[:, us, HB:HB + TW], in_=ps[:, bu, :], func=AF.Copy),
              reads=[("ps", bu)], writes=[("cv", us)])
        S.add("act", lambda e: e.activation(out=halo_b[:, bl, c, 0:15], in_=cvb[:, us, HB + TW - 15:HB + TW],
                                            func=AF.Copy),
              reads=[("cv", us)], writes=[("halo_b", bl, c)])
        S.add("act", lambda e: e.activation(out=szb[:, zs, :], in_=ps[:, bz, :], func=AF.Silu),
              reads=[("ps", bz)], writes=[("sz", zs)])
        nstg = {2: 1, 4: 2, 8: 3, 16: 4}[w]
        prev, prev_keys = (lambda lo, hi: cvb[:, us, lo:hi]), [("cv", us), ("cvh", us)]
        for st_ in range(1, nstg + 1):
            lo = HB - (w - 2 ** st_)
            sh = 2 ** (st_ - 1)
            pp = (st_ - 1) % 2
            S.add("dve", lambda e, lo=lo, sh=sh, pp=pp, prev=prev: e.tensor_tensor(
                out=sab[:, pp, lo:HB + TW], in0=prev(lo, HB + TW), in1=prev(lo - sh, HB + TW - sh), op=ALU.add),
                reads=prev_keys, writes=[("sa", pp)])
            prev, prev_keys = (lambda lo, hi, pp=pp: sab[:, pp, lo:hi]), [("sa", pp)]
        S.add("dve", lambda e, prev=prev: e.scalar_tensor_tensor(
            out=dT[:, ds, :], in0=prev(HB, HB + TW), scalar=1.0 / w, in1=cvb[:, us, HB:HB + TW],
            op0=ALU.mult, op1=ALU.subtract),
            reads=prev_keys + [("cv", us)], writes=[("d", ds)])
        if first_tile:
            wi = POOL_WINDOWS.index(w)
            S.add("dve", lambda e, prev=prev: e.tensor_tensor(out=fix[:, 0:15], in0=prev(HB, HB + 15),
                                                              in1=invc[:, wi, 0:15], op=ALU.mult),
                  reads=prev_keys + [("invcw", wi)], writes=[("fix",)])
            S.add("dve", lambda e: e.tensor_tensor(out=dT[:, ds, 0:15], in0=fix[:, 0:15], in1=cvb[:, us, HB:HB + 15],
                                                   op=ALU.subtract),
                  reads=[("fix",), ("cv", us)], writes=[("d", ds)])
        return ds, zs

    def stage_B_grp(i, g, j, dss, zss):
        bl = i // 2
        for oc in range(2):
            b = bank()
            for ic in range(2):
                S.add("pe", lambda e, oc=oc, ic=ic, b=b: e.matmul(
                    ps[:, b, :], lhsT=wgrp[:, g, ic, oc * 128:(oc + 1) * 128], rhs=dT[:, dss[ic], :],
                    start=(ic == 0), stop=(ic == 1)),
                    reads=[("wgrp",), ("d", dss[ic])], writes=[("ps", b)])
            c = 2 * g + oc
            S.add("dve", lambda e, b=b, c=c, oc=oc: e.scalar_tensor_tensor(
                out=mT[:, c, tsl(j)], in0=ps[:, b, :], scalar=pcol(PC_SCALE + bl * KC + c), in1=szb[:, zss[oc], :],
                op0=ALU.mult, op1=ALU.mult),
                reads=[("ps", b), ("sz", zss[oc]), ("par",)], writes=[("m", c, j)])

    def stage_out(i, j, hooks=None):
        korder = chunk_order(i)
        for o in range(KC):
            b = bank()
            for n_, k in enumerate(korder):
                S.add("pe", lambda e, o=o, k=k, b=b, n_=n_: e.matmul(
                    ps[:, b, :], lhsT=wout[:, k, o * 128:(o + 1) * 128], rhs=mT[:, k, tsl(j)],
                    start=(n_ == 0), stop=(n_ == KC - 1)),
                    reads=[("wout", k // 4), ("m", k, j)], writes=[("ps", b)])
            S.add("dve", lambda e, o=o, b=b: e.tensor_tensor(out=hT[:, o, tsl(j)], in0=hT[:, o, tsl(j)],
                                                            in1=ps[:, b, :], op=ALU.add),
                  reads=[("h", o, j), ("ps", b)], writes=[("h", o, j)])
            if hooks and o in hooks:
                hooks[o]()

    def stage_ple(i, j, hooks=None):
        for o in range(KC):
            bg, bp = bank(), bank()
            for k2 in range(2):
                S.add("pe", lambda e, o=o, k2=k2, bp=bp: e.matmul(ps[:, bp, :],
                                                                 lhsT=wproj[:, k2, o * 128:(o + 1) * 128],
                                                                 rhs=pT[:, j, k2, :], start=(k2 == 0),
                                                                 stop=(k2 == 1)),
                      reads=[("wproj",), ("pT", j)], writes=[("ps", bp)])
            for k in range(KC):
                S.add("pe", lambda e, o=o, k=k, bg=bg: e.matmul(ps[:, bg, :], lhsT=wgate[:, k, o * 128:(o + 1) * 128],
                                                               rhs=hn[:, k, tsl(j)], start=(k == 0),
                                                               stop=(k == KC - 1)),
                      reads=[("wgate", o // 4), ("hn", k, j)], writes=[("ps", bg)])
            ts_ = nxt("t", 2)
            ms = nxt("tmp", 2)
            S.add("act", lambda e, bg=bg, ts_=ts_: e.activation(out=tb[:, ts_, :], in_=ps[:, bg, :], func=AF.Tanh,
                                                               scale=0.5),
                  reads=[("ps", bg)], writes=[("t", ts_)])
            S.add("dve", lambda e, bp=bp, ts_=ts_, ms=ms: e.scalar_tensor_tensor(
                out=tmpb[:, ms, :], in0=tb[:, ts_, :], scalar=1.0, in1=ps[:, bp, :], op0=ALU.add, op1=ALU.mult),
                reads=[("t", ts_), ("ps", bp)], writes=[("tmp", ms)])
            S.add("dve", lambda e, o=o, ms=ms: e.scalar_tensor_tensor(
                out=hT[:, o, tsl(j)], in0=tmpb[:, ms, :], scalar=0.5, in1=hT[:, o, tsl(j)],
                op0=ALU.mult, op1=ALU.add),
                reads=[("tmp", ms), ("h", o, j)], writes=[("h", o, j)])
            if hooks and o in hooks:
                hooks[o]()

    fin = {"o": 0}

    def stage_T(sup, j, blocks=(0, 1, 2, 3), act_r=False):
        for n in blocks:
            b = bank2()
            pb = pair(b)
            c0 = j * TW + n * 128
            for k in range(KC):
                S.add("pe", lambda e, k=k, pb=pb, c0=c0: e.transpose(
                    out=pb[:, k * 128:(k + 1) * 128], in_=hT[:, k, c0:c0 + 128], identity=ident[:]),
                    reads=[("h", k, j), ("ident",)], writes=[("ps", b), ("ps", b + 1)])
            os_ = fin["o"] % 4
            cc = fin["o"] % 8
            fin["o"] += 1
            S.add("act", lambda e, pb=pb, cc=cc: e.activation(out=junk[:], in_=pb, func=AF.Square,
                                                              accum_out=ssq[:, cc:cc + 1]),
                  reads=[("ps", b), ("ps", b + 1)], writes=[("junk",), ("ssq", cc)])
            if act_r:
                S.add("act", lambda e, cc=cc: e.activation(out=rt[:, cc:cc + 1], in_=ssq[:, cc:cc + 1], func=AF.Ln,
                                                           bias=cbias[:, 0:1], scale=1.0),
                      reads=[("ssq", cc), ("cbias",)], writes=[("rt", cc)])
                S.add("act", lambda e, cc=cc: e.activation(out=rt[:, cc:cc + 1], in_=rt[:, cc:cc + 1], func=AF.Exp,
                                                           scale=-0.5),
                      reads=[("rt", cc)], writes=[("rt", cc)])
            else:
                S.add("pool", lambda e, cc=cc: e.tensor_tensor(out=rt[:, cc:cc + 1], in0=ssq[:, cc:cc + 1],
                                                               in1=cst[:, 0:1], op=ALU.add),
                      reads=[("ssq", cc), ("cst",)], writes=[("rt", cc)])
                S.add("pool", lambda e, cc=cc: e.tensor_tensor(out=rt[:, cc:cc + 1], in0=rt[:, cc:cc + 1],
                                                               in1=cst[:, 1:2], op=ALU.pow),
                      reads=[("rt", cc), ("cst",)], writes=[("rt", cc)])
            S.add("dve", lambda e, pb=pb, os_=os_, cc=cc: e.scalar_tensor_tensor(
                out=ost[:, os_, :], in0=pb, scalar=rt[:, cc:cc + 1], in1=gbc[:], op0=ALU.mult, op1=ALU.mult),
                reads=[("ps", b), ("ps", b + 1), ("rt", cc), ("gbc",)], writes=ost_keys(os_))
            r0 = sup * SUP + j * TW + n * 128
            S.add("sp", lambda e, os_=os_, r0=r0: e.dma_start(out=y_d[r0:r0 + 128, :], in_=ost[:, os_, :]),
                  reads=ost_keys(os_), writes=[("y", r0)], dma=("ost", os_))

    for blk in range(4):
        load_x_block(0, blk)
    load_p(0, 0)
    ensure_pieces(2)
    loads_wout(0)
    pst_ = {"pidx": 0}

    def piece_done():
        pst_["pidx"] += 1
        ensure_pieces(pst_["pidx"] + 2)

    def mixer_phase(i, sup, n1_tile1, smalls):
        order = chunk_order(i)
        first = (sup == 0)

        def small():
            if smalls:
                smalls.pop(0)()
        if i % 2 == 0:
            c0, c1 = order[0], order[1]
            s0, s1 = pst_["pidx"] % 3, (pst_["pidx"] + 1) % 3
            small()
            n1_tile1.a()
            stage_A(i, c0, 0, s0, kouter=True)
            n1_tile1.b()
            small()
            stage_A(i, c1, 0, s1)
            stage_A(i, c0, 1, s0, kouter=True)
            piece_done()
            stage_A(i, c1, 1, s1)
            piece_done()
            for c in order[2:]:
                slot = pst_["pidx"] % 3
                small()
                stage_A(i, c, 0, slot)
                stage_A(i, c, 1, slot)
                piece_done()
        else:
            res = {}
            pend = []

            def uz(c, j, slot, kouter=False):
                res[(c, j)] = stage_B_uz(i, c, j, slot, first and j == 0, kouter)
                while len(pend) > 0 and pend[0][2] < uz.n:
                    g, jj, _ = pend.pop(0)
                    emit_grp(g, jj)
                uz.n += 1
                g = c // 2
                if (2 * g, j) in res and (2 * g + 1, j) in res:
                    pend.append((g, j, uz.n))
            uz.n = 0

            def emit_grp(g, j):
                a, b_ = res[(2 * g, j)], res[(2 * g + 1, j)]
                stage_B_grp(i, g, j, (a[0], b_[0]), (a[1], b_[1]))
            c0, c1 = order[0], order[1]
            s0, s1 = pst_["pidx"] % 3, (pst_["pidx"] + 1) % 3
            small()
            n1_tile1.a()
            uz(c0, 0, s0, kouter=True)
            n1_tile1.b()
            small()
            uz(c1, 0, s1)
            uz(c0, 1, s0, kouter=True)
            piece_done()
            uz(c1, 1, s1)
            piece_done()
            for c in order[2:]:
                slot = pst_["pidx"] % 3
                small()
                uz(c, 0, slot)
                uz(c, 1, slot)
                piece_done()
            while pend:
                g, jj, _ = pend.pop(0)
                emit_grp(g, jj)
        while smalls:
            smalls.pop(0)()

    for sup in range(nsup):
        if sup == 0:
            stage_X(sup, 0)
            Norm(0, PC_NMIX + 0 * KC).both()
            stage_X(sup, 1)
        if sup == 0:
            dump("x")
        for i in range(DEPTH):
            smalls = small_loads(i, sup)
            for j in range(NT):
                stage_P(i, j)
            mixer_phase(i, sup, Norm(1, PC_NMIX + i * KC), smalls)
            if sup == 0:
                dump("m%d" % i, "m")
            n2_0 = Norm(0, PC_NPLE + i * KC)
            n2_1 = Norm(1, PC_NPLE + i * KC)
            last = (i == DEPTH - 1)
            more = sup + 1 < nsup
            stage_out(i, 0)
            load_p_next(i, sup)
            stage_out(i, 1, hooks={0: n2_0.a, 3: n2_0.b})
            load_wout_next(i, sup)
            if last and more:
                for blk in range(4):
                    load_x_block(sup + 1, blk)
            if sup == 0:
                dump("mix%d" % i)
            stage_ple(i, 0, hooks={0: n2_1.a, 2: n2_1.b})
            if not last:
                nx0 = Norm(0, PC_NMIX + (i + 1) * KC)
                stage_ple(i, 1, hooks={0: nx0.a, 2: nx0.b})
            else:
                hk = {0: (lambda: stage_T(sup, 0, (0,))), 1: (lambda: stage_T(sup, 0, (1,))),
                      2: (lambda: stage_T(sup, 0, (2,))), 3: (lambda: stage_T(sup, 0, (3,)))}
                if more:
                    hk[4] = lambda: stage_X(sup + 1, 0, (0, 1))
                    hk[5] = lambda: stage_X(sup + 1, 0, (2, 3))
                stage_ple(i, 1, hooks=hk)
            if sup == 0:
                dump("L%d" % i)
        if sup + 1 < nsup:
            n10 = Norm(0, PC_NMIX + 0 * KC)
            n10.a()
            stage_T(sup, 1, act_r=True)
            n10.b()
            stage_X(sup + 1, 1)
        else:
            stage_T(sup, 1, act_r=True)

    S.finalize()
    sems = {}
    for eng in Sched.ENGS:
        sems[("eng", eng)] = es.enter_context(nc.semaphore("c_" + eng))
    for n_, k in enumerate(S.dma_keys):
        sems[("dma", k)] = es.enter_context(nc.semaphore("d%d" % n_))

    with nc.Block() as block:
        @block.tensor
        def _(e):
            S.emit_engine("pe", e, sems)

        @block.scalar
        def _(e):
            S.emit_engine("act", e, sems)

        @block.vector
        def _(e):
            S.emit_engine("dve", e, sems)

        @block.gpsimd
        def _(e):
            S.emit_engine("pool", e, sems)

        @block.sync
        def _(e):
            waited = S.emit_engine("sp", e, sems)
            for k, v in S.final_dma.items():
                if k[0] in ("ost", "dbg") and waited.get(("dma", k), 0) < v:
                    e.wait_ge(sems[("dma", k)], v)
    es.close()
    return nc


def feature_major(v):
    v = np.asarray(v, dtype=np.float32).reshape(-1, KC, 128)
    return np.ascontiguousarray(v.transpose(2, 0, 1).reshape(128, -1))


def make_params(norm_mix, ple_norm, final_norm, a_w_conv, b_scale):
    cols = [feature_major(norm_mix), feature_major(ple_norm), feature_major(final_norm),
            feature_major(a_w_conv), feature_major(b_scale)]
    par = np.ascontiguousarray(np.concatenate(cols, axis=1))
    assert par.shape == (128, NPAR), par.shape
    return par


_NC_CACHE = {}


def kernel(x, p, norm_mix, a_w_in, a_w_conv, a_w_out, b_w_in, b_w_grp, b_scale, b_w_out,
           ple_norm, ple_w_gate, ple_w_proj, final_norm):
    n = 8
    if "nc" not in _NC_CACHE:
        _NC_CACHE["nc"] = build_program()
    nc = _NC_CACHE["nc"]
    par = make_params(norm_mix, ple_norm, final_norm, a_w_conv, b_scale)
    shared = {
        "params": par,
        "gfin": np.ascontiguousarray(np.asarray(final_norm, dtype=np.float32).reshape(1, D)),
        "a_w_in": np.ascontiguousarray(a_w_in, dtype=np.float32),
        "a_w_out": np.ascontiguousarray(a_w_out, dtype=np.float32),
        "b_w_in": np.ascontiguousarray(b_w_in, dtype=np.float32),
        "b_w_grp": np.ascontiguousarray(b_w_grp, dtype=np.float32),
        "b_w_out": np.ascontiguousarray(b_w_out, dtype=np.float32),
        "ple_w_gate": np.ascontiguousarray(ple_w_gate, dtype=np.float32),
        "ple_w_proj": np.ascontiguousarray(ple_w_proj, dtype=np.float32),
    }
    in_maps = []
    for b in range(n):
        m = dict(shared)
        m["x"] = np.ascontiguousarray(x[b], dtype=np.float32)
        m["p"] = np.ascontiguousarray(p[:, b], dtype=np.float32)
        in_maps.append(m)
    res = run_bass_kernel_spmd(nc, in_maps, core_ids=list(range(n)))
    return np.stack([np.asarray(r["y"], dtype=np.float32) for r in res.results], axis=0)
```

```python
import numpy as np
from contextlib import ExitStack

import concourse.bass as bass
import concourse.mybir as mybir
from concourse.bass_utils import run_bass_kernel_spmd

F32 = mybir.dt.float32
BF16 = mybir.dt.bfloat16
AF = mybir.ActivationFunctionType
ALU = mybir.AluOpType

D = 1024
SEQ = 4096
DEPTH = 4
PLE = 256
EPS = 1e-6
KC = 8
TW = 512
NT = 2
SUP = TW * NT
POOL_WINDOWS = (2, 4, 8, 16)
HB = 16

PC_NMIX = 0
PC_NPLE = 32
PC_NFIN = 64
PC_CONV = 72
PC_SCALE = 120
NPAR = 136
NG32 = 72


class Sched:
    ENGS = ("pe", "act", "dve", "pool", "sp")

    def __init__(self):
        self.ops = []
        self.last_w = {}
        self.readers = {}
        self.eng_ops = {e: [] for e in self.ENGS}
        self.dma_keys = []

    def add(self, eng, fn, reads=(), writes=(), dma=None, ninst=1):
        oid = len(self.ops)
        deps = set()
        for r in reads:
            w = self.last_w.get(r)
            if w is not None:
                deps.add(w)
        for w_ in writes:
            w = self.last_w.get(w_)
            if w is not None:
                deps.add(w)
            for rd in self.readers.get(w_, ()):
                deps.add(rd)
        deps.discard(oid)
        for r in reads:
            self.readers.setdefault(r, []).append(oid)
        for w_ in writes:
            self.last_w[w_] = oid
            self.readers[w_] = []
        op = dict(id=oid, eng=eng, fn=fn, deps=deps, dma=dma, ninst=ninst,
                  seq=len(self.eng_ops[eng]), signal=False, tok=None)
        if dma is not None and dma not in self.dma_keys:
            self.dma_keys.append(dma)
        self.ops.append(op)
        self.eng_ops[eng].append(op)
        return oid

    def _needed(self, op, d):
        if d["dma"] is not None:
            return True
        if d["eng"] != op["eng"]:
            return True
        if op["eng"] == "pe":
            return False
        return True

    def finalize(self):
        for op in self.ops:
            for did in op["deps"]:
                d = self.ops[did]
                if d["dma"] is None and self._needed(op, d):
                    d["signal"] = True
        cnt = {e: 0 for e in self.ENGS}
        dcnt = {k: 0 for k in self.dma_keys}
        for op in self.ops:
            if op["dma"] is not None:
                dcnt[op["dma"]] += 16 * op["ninst"]
                op["tok"] = (("dma", op["dma"]), dcnt[op["dma"]])
            elif op["signal"]:
                cnt[op["eng"]] += 1
                op["tok"] = (("eng", op["eng"]), cnt[op["eng"]])
        self.final_dma = dict(dcnt)
        self.final_cnt = dict(cnt)

    def emit_engine(self, eng, e, sems):
        waited = {}
        attach = eng != "pe"
        for op in self.eng_ops[eng]:
            need = {}
            for did in op["deps"]:
                d = self.ops[did]
                if not self._needed(op, d):
                    continue
                key, val = d["tok"]
                if waited.get(key, 0) >= val:
                    continue
                if need.get(key, 0) < val:
                    need[key] = val
            items = list(need.items())
            sep = items[:-1] if attach else items
            for key, val in sep:
                e.wait_ge(sems[key], val)
                waited[key] = val
            insts = op["fn"](e)
            if not isinstance(insts, (list, tuple)):
                insts = [insts]
            if attach and items:
                key, val = items[-1]
                insts[0]._wait_ge(sems[key], val)
                waited[key] = val
            if op["dma"] is not None:
                assert len(insts) == op["ninst"], (len(insts), op["ninst"])
                for ins in insts:
                    ins.then_inc(sems[("dma", op["dma"])], 16)
            elif op["signal"]:
                insts[-1].then_inc(sems[("eng", eng)], 1)
        return waited


def chunk_order(i):
    return list(range(KC)) if i % 2 == 0 else [6, 7, 4, 5, 2, 3, 0, 1]


def build_program(nsup=SEQ // SUP, ntok=SEQ, debug=()):
    nc = bass.Bass("TRN2", target_bir_lowering=False)
    S = Sched()
    if debug:
        dbg_d = nc.dram_tensor("dbg", [len(debug), 128, KC * SUP], F32, kind="ExternalOutput").ap()

    x_d = nc.dram_tensor("x", [ntok, D], F32, kind="ExternalInput").ap()
    p_d = nc.dram_tensor("p", [DEPTH, ntok, PLE], F32, kind="ExternalInput").ap()
    par_d = nc.dram_tensor("params", [128, NPAR], F32, kind="ExternalInput").ap()
    awin_d = nc.dram_tensor("a_w_in", [2, D, 4 * D], F32, kind="ExternalInput").ap()
    awout_d = nc.dram_tensor("a_w_out", [2, D, D], F32, kind="ExternalInput").ap()
    bwin_d = nc.dram_tensor("b_w_in", [2, D, 2 * D], F32, kind="ExternalInput").ap()
    bwgrp_d = nc.dram_tensor("b_w_grp", [2, 4, 256, 256], F32, kind="ExternalInput").ap()
    bwout_d = nc.dram_tensor("b_w_out", [2, D, D], F32, kind="ExternalInput").ap()
    wgate_d = nc.dram_tensor("ple_w_gate", [DEPTH, D, D], F32, kind="ExternalInput").ap()
    wproj_d = nc.dram_tensor("ple_w_proj", [DEPTH, PLE, D], F32, kind="ExternalInput").ap()
    gfin_d = nc.dram_tensor("gfin", [1, D], F32, kind="ExternalInput").ap()
    y_d = nc.dram_tensor("y", [ntok, D], F32, kind="ExternalOutput").ap()

    es = ExitStack()

    def sb(name, shape, dt):
        return es.enter_context(nc.sbuf_tensor(name, shape, dt))

    NSZ = 8
    ND = 8
    hT = sb("hT", [128, KC, SUP], F32)
    hn = sb("hn", [128, KC, SUP], BF16)
    mT = sb("mT", [128, KC, SUP], BF16)
    win = sb("win", [128, 3, KC, 4, 128], BF16)
    wout = sb("wout", [128, KC, D], BF16)
    wgate = sb("wgate", [128, KC, D], BF16)
    wproj = sb("wproj", [128, 2, D], BF16)
    wgrp = sb("wgrp", [128, 4, 2, 256], BF16)
    pT = sb("pT", [128, NT, 2, TW], BF16)
    pst = sb("pst", [128, SUP // 128, PLE], BF16)
    sq = sb("sq", [128, KC, TW], BF16)
    rr = sb("rr", [128, 2, TW], F32)
    szb = sb("szb", [128, NSZ, TW], F32)
    cgb = sb("cgb", [128, 2, TW], F32)
    cvb = sb("cvb", [128, 2, HB + TW], F32)
    yb = sb("yb", [128, 2, TW], F32)
    tb = sb("tb", [128, 2, TW], F32)
    tmpb = sb("tmpb", [128, 2, TW], F32)
    sab = sb("sab", [128, 2, HB + TW], F32)
    dT = sb("dT", [128, ND, TW], BF16)
    fix = sb("fix", [128, 16], F32)
    cbias = sb("cbias", [128, 1], F32)
    par = sb("par", [128, NPAR], F32)
    ident = sb("ident", [128, 128], F32)
    identb = sb("identb", [128, 128], BF16)
    ones = sb("ones", [128, 128], BF16)
    invc = sb("invc", [128, 5, 16], F32)
    halo_a = sb("halo_a", [128, 2, KC, 2], F32)
    halo_b = sb("halo_b", [128, 2, KC, 16], F32)
    gbc = sb("gbc", [128, D], F32)
    junk = sb("junk", [128, D], BF16)
    ssq = sb("ssq", [128, 8], F32)
    rt = sb("rt", [128, 8], F32)
    cst = sb("cst", [128, 2], F32)
    ps = es.enter_context(nc.psum_tensor("ps", [128, 8, TW], F32))

    xst = mT[:].bitcast(F32)
    ost = szb[:, :, :].rearrange("p (s a) t -> p s (a t)", a=2)

    def xst_slot(s):
        return xst[:, 2 * s:2 * s + 2, :].rearrange("p a b -> p (a b)")

    def xst_keys(s):
        return [("m", 2 * s + a, j) for a in (0, 1) for j in range(NT)]

    def ost_keys(s):
        return [("sz", 2 * s), ("sz", 2 * s + 1)]

    def dump(name, kind="h"):
        if name not in debug:
            return
        idx = list(debug).index(name)
        src, key = {"h": (hT, "h"), "hn": (hn, "hn"), "m": (mT, "m")}[kind]
        dst = dbg_d[idx] if kind == "h" else dbg_d[idx].bitcast(BF16)[:, 0:KC * SUP]
        S.add("sp", lambda e: e.dma_start(out=dst, in_=src[:].rearrange("p k t -> p (k t)")),
              reads=[(key, k, j) for k in range(KC) for j in range(NT)], writes=[("dbg", idx)], dma=("dbg", idx))

    ring = {"n": 0}

    def bank():
        b = ring["n"] % 8
        ring["n"] += 1
        return b

    def bank2():
        if ring["n"] % 2:
            ring["n"] += 1
        b = ring["n"] % 8
        ring["n"] += 2
        return b

    def pair(b):
        return ps[:, b:b + 2, :].rearrange("p a b -> p (a b)")

    def tsl(j):
        return slice(j * TW, (j + 1) * TW)

    def pcol(c):
        return par[:, c:c + 1]

    S.add("sp", lambda e: e.dma_start(out=par[:], in_=par_d[:, :]), writes=[("par",)], dma="par")
    S.add("pool", lambda e: e.memset(ident[:], 0.0), writes=[("ident",)])
    S.add("pool", lambda e: e.affine_select(out=ident[:], in_=ident[:], compare_op=ALU.not_equal, fill=1.0,
                                            base=0, pattern=[[-1, 128]], channel_multiplier=1),
          reads=[("ident",)], writes=[("ident",)])
    S.add("pool", lambda e: e.tensor_copy(out=identb[:], in_=ident[:]), reads=[("ident",)], writes=[("identb",)])
    S.add("pool", lambda e: e.memset(ones[:], 1.0), writes=[("ones",)])
    S.add("pool", lambda e: e.memset(halo_a[:], 0.0), writes=[("halo_a", a, c) for a in range(2) for c in range(KC)])
    S.add("pool", lambda e: e.memset(halo_b[:], 0.0), writes=[("halo_b", a, c) for a in range(2) for c in range(KC)])

    def mk_inv(t):
        return lambda e: e.memset(invc[:, 4, t:t + 1], 1.0 / (t + 1))
    for t in range(16):
        S.add("pool", mk_inv(t), writes=[("invc",)])

    def mk_invw(wi, w):
        return lambda e: e.tensor_scalar(out=invc[:, wi, :], in0=invc[:, 4, :], scalar1=1.0 / w, scalar2=None,
                                         op0=ALU.max)
    for wi, w in enumerate(POOL_WINDOWS):
        S.add("dve", mk_invw(wi, w), reads=[("invc",)], writes=[("invcw", wi)])
    S.add("pool", lambda e: e.memset(cbias[:], float(D * EPS)), writes=[("cbias",)])
    S.add("sp", lambda e: e.dma_start(out=gbc[:], in_=gfin_d[0:1, :].to_broadcast([128, D])),
          writes=[("gbc",)], dma="gbc")
    S.add("dve", lambda e: e.tensor_scalar(out=gbc[:], in0=gbc[:], scalar1=32.0, scalar2=None, op0=ALU.mult),
          reads=[("gbc",)], writes=[("gbc",)])
    S.add("pool", lambda e: e.memset(cst[:, 0:1], float(D * EPS)), writes=[("cst",)])
    S.add("pool", lambda e: e.memset(cst[:, 1:2], -0.5), writes=[("cst",)])
    S.add("dve", lambda e: e.tensor_scalar(out=par[:, 0:NG32], in0=par[:, 0:NG32], scalar1=32.0, scalar2=None,
                                           op0=ALU.mult), reads=[("par",)], writes=[("par",)])

    pieces = []
    for s_ in range(nsup):
        for i in range(DEPTH):
            for c in chunk_order(i):
                pieces.append((i, c))
    st = {"next_piece": 0}

    def load_piece(n):
        i, c = pieces[n]
        slot = n % 3
        if i % 2 == 0:
            src = awin_d[i // 2].rearrange("(k p) (g c e) -> p k g c e", p=128, g=4, c=KC)
            ng = 4
        else:
            src = bwin_d[i // 2].rearrange("(k p) (g c e) -> p k g c e", p=128, g=2, c=KC)
            ng = 2

        def fn(e):
            return [e.dma_start(out=win[:, slot, :, g, :], in_=src[:, :, g, c, :]) for g in range(ng)]
        S.add("pool", fn, writes=[("win", slot)], dma=("win", slot), ninst=ng)

    def ensure_pieces(upto):
        while st["next_piece"] <= min(upto, len(pieces) - 1):
            load_piece(st["next_piece"])
            st["next_piece"] += 1

    def load_p(i, sup):
        src = p_d[i, sup * SUP:(sup + 1) * SUP, :].rearrange("(n p) d -> p n d", p=128)
        S.add("pool", lambda e: e.dma_start(out=pst[:, :, :], in_=src), writes=[("pst",)], dma=("pst",))

    def loads_wout(i):
        wo_src = (awout_d[i // 2] if i % 2 == 0 else bwout_d[i // 2]).rearrange("(k p) n -> p k n", p=128)
        if i % 2 == 1:
            def fn(e):
                return [e.dma_start(out=wgrp[:, g, :, :],
                                    in_=bwgrp_d[i // 2, g].rearrange("(ic p) n -> p ic n", p=128))
                        for g in range(4)]
            S.add("pool", fn, writes=[("wgrp",)], dma=("wgrp",), ninst=4)
        for h in range(2):
            ks = slice(4 * h, 4 * h + 4)
            S.add("pool", lambda e, ks=ks: e.dma_start(out=wout[:, ks, :], in_=wo_src[:, ks, :]),
                  writes=[("wout", h)], dma=("wout", h))

    def small_loads(i, sup):
        out = []
        wg_src = wgate_d[i].rearrange("(k p) n -> p k n", p=128)
        wp_src = wproj_d[i].rearrange("(k p) n -> p k n", p=128)
        for h in range(2):
            cs = slice(512 * h, 512 * h + 512)
            out.append(lambda cs=cs, h=h: S.add(
                "pool", lambda e: e.dma_start(out=wgate[:, :, cs], in_=wg_src[:, :, cs]),
                writes=[("wgate", h)], dma=("wgate", h)))
        out.append(lambda: S.add("pool", lambda e: e.dma_start(out=wproj[:], in_=wp_src),
                                 writes=[("wproj",)], dma=("wproj",)))
        return out

    def load_p_next(i, sup):
        ni, nsup_ = (i + 1, sup) if i + 1 < DEPTH else (0, sup + 1)
        if nsup_ < nsup:
            load_p(ni, nsup_)

    def load_wout_next(i, sup):
        ni, nsup_ = (i + 1, sup) if i + 1 < DEPTH else (0, sup + 1)
        if nsup_ < nsup:
            loads_wout(ni)

    def load_x_block(sup, blk):
        slot = blk % 4
        r0 = sup * SUP + blk * 128
        S.add("sp", lambda e: e.dma_start(out=xst_slot(slot), in_=x_d[r0:r0 + 128, :]),
              writes=xst_keys(slot), dma=("xst", slot))

    def stage_X(sup, j, blocks=(0, 1, 2, 3)):
        for n in blocks:
            blk = j * 4 + n
            slot = blk % 4
            b = bank2()
            pv = pair(b).rearrange("p (k t) -> p k t", t=128)
            for k in range(KC):
                S.add("pe", lambda e, k=k, pv=pv, slot=slot: e.transpose(
                    out=pv[:, k, :], in_=xst_slot(slot)[:, k * 128:(k + 1) * 128], identity=ident[:]),
                    reads=xst_keys(slot) + [("ident",)], writes=[("ps", b), ("ps", b + 1)])
            c0 = j * TW + n * 128
            S.add("act", lambda e, pv=pv, c0=c0: e.activation(out=hT[:, :, c0:c0 + 128], in_=pv, func=AF.Copy),
                  reads=[("ps", b), ("ps", b + 1)], writes=[("h", k, j) for k in range(KC)])
            if blk + 4 < SUP // 128:
                load_x_block(sup, blk + 4)

    nst = {"r": 0}

    def norm_sq(j):
        for k in range(KC):
            S.add("act", lambda e, k=k: e.activation(out=sq[:, k, :], in_=hT[:, k, tsl(j)], func=AF.Square),
                  reads=[("h", k, j)], writes=[("sq", k)])

    def norm_stats(j):
        b = bank()
        for k in range(KC):
            S.add("pe", lambda e, k=k: e.matmul(ps[:, b, :], lhsT=ones[:], rhs=sq[:, k, :],
                                                start=(k == 0), stop=(k == KC - 1)),
                  reads=[("sq", k), ("ones",)], writes=[("ps", b)])
        rs = nst["r"] % 2
        nst["r"] += 1
        S.add("act", lambda e: e.activation(out=rr[:, rs, :], in_=ps[:, b, :], func=AF.Ln, bias=cbias[:, 0:1],
                                            scale=1.0),
              reads=[("ps", b), ("cbias",)], writes=[("rr", rs)])
        S.add("act", lambda e: e.activation(out=rr[:, rs, :], in_=rr[:, rs, :], func=AF.Exp, scale=-0.5),
              reads=[("rr", rs)], writes=[("rr", rs)])
        return rs

    def norm_apply(j, rs, gcol, dst, dst_keys):
        for k in range(KC):
            S.add("dve", lambda e, k=k: e.scalar_tensor_tensor(
                out=dst(k), in0=hT[:, k, tsl(j)], scalar=pcol(gcol + k), in1=rr[:, rs, :],
                op0=ALU.mult, op1=ALU.mult),
                reads=[("h", k, j), ("rr", rs), ("par",)], writes=dst_keys(k))

    class Norm:
        def __init__(self, j, gcol):
            self.j, self.gcol = j, gcol

        def a(self):
            norm_sq(self.j)

        def b1(self):
            self.rs = norm_stats(self.j)

        def b2(self):
            j, rs = self.j, self.rs
            norm_apply(j, rs, self.gcol, (lambda k: hn[:, k, tsl(j)]), (lambda k: [("hn", k, j)]))

        def b(self):
            self.b1()
            self.b2()

        def both(self):
            self.a()
            self.b()

    cnt = {"sz": 0, "cg": 0, "cv": 0, "y": 0, "t": 0, "tmp": 0, "d": 0}

    def nxt(name, n):
        v = cnt[name] % n
        cnt[name] += 1
        return v

    def stage_P(i, j):
        b = bank()
        pv = ps[:, b, :].bitcast(BF16).rearrange("p (k t) -> p k t", k=2)
        for k2 in range(2):
            for n in range(4):
                S.add("pe", lambda e, k2=k2, n=n: e.transpose(
                    out=pv[:, k2, n * 128:(n + 1) * 128], in_=pst[:, j * 4 + n, k2 * 128:(k2 + 1) * 128],
                    identity=identb[:]),
                    reads=[("pst",), ("identb",)], writes=[("ps", b)])
        S.add("act", lambda e: e.activation(out=pT[:, j, :, :], in_=pv, func=AF.Copy),
              reads=[("ps", b)], writes=[("pT", j)])

    def mm_group(banks, lhs_of, j, kouter):
        ng = len(banks)
        order = [(g, k) for k in range(KC) for g in range(ng)] if kouter else \
                [(g, k) for g in range(ng) for k in range(KC)]
        for g, k in order:
            lhs, rd = lhs_of(g, k)
            S.add("pe", lambda e, g=g, k=k, lhs=lhs: e.matmul(ps[:, banks[g], :], lhsT=lhs, rhs=hn[:, k, tsl(j)],
                                                             start=(k == 0), stop=(k == KC - 1)),
                  reads=rd + [("hn", k, j)], writes=[("ps", banks[g])])

    def stage_A(i, c, j, slot, kouter=False):
        al = i // 2
        bs = [bank() for _ in range(4)]
        gmap = (1, 2, 0, 3)
        mm_group(bs, lambda g, k: (win[:, slot, k, gmap[g], :], [("win", slot)]), j, kouter)
        b_cg, b_v, b_bg, b_z = bs
        zs = nxt("sz", NSZ)
        gs = nxt("cg", 2)
        vs = nxt("cv", 2)
        ys = nxt("y", 2)
        w0 = PC_CONV + (al * 3 + 0) * KC + c
        w1 = PC_CONV + (al * 3 + 1) * KC + c
        w2 = PC_CONV + (al * 3 + 2) * KC + c
        S.add("act", lambda e: e.activation(out=cgb[:, gs, :], in_=ps[:, b_cg, :], func=AF.Copy),
              reads=[("ps", b_cg)], writes=[("cg", gs)])
        S.add("act", lambda e: e.activation(out=cvb[:, vs, HB - 2:HB], in_=halo_a[:, al, c, :], func=AF.Copy),
              reads=[("halo_a", al, c)], writes=[("cvh", vs)])
        S.add("dve", lambda e: e.tensor_tensor(out=cvb[:, vs, HB:HB + TW], in0=cgb[:, gs, :], in1=ps[:, b_v, :],
                                               op=ALU.mult),
              reads=[("cg", gs), ("ps", b_v)], writes=[("cv", vs)])
        S.add("act", lambda e: e.activation(out=halo_a[:, al, c, :], in_=cvb[:, vs, HB + TW - 2:HB + TW],
                                            func=AF.Copy),
              reads=[("cv", vs)], writes=[("halo_a", al, c)])
        S.add("act", lambda e: e.activation(out=yb[:, ys, :], in_=cvb[:, vs, HB - 2:HB - 2 + TW], func=AF.Identity,
                                            scale=pcol(w0)),
              reads=[("cv", vs), ("cvh", vs), ("par",)], writes=[("y", ys)])
        S.add("dve", lambda e: e.scalar_tensor_tensor(out=yb[:, ys, :], in0=cvb[:, vs, HB - 1:HB - 1 + TW],
                                                      scalar=pcol(w1), in1=yb[:, ys, :], op0=ALU.mult, op1=ALU.add),
              reads=[("cv", vs), ("cvh", vs), ("y", ys), ("par",)], writes=[("y", ys)])
        S.add("dve", lambda e: e.scalar_tensor_tensor(out=yb[:, ys, :], in0=cvb[:, vs, HB:HB + TW],
                                                      scalar=pcol(w2), in1=yb[:, ys, :], op0=ALU.mult, op1=ALU.add),
              reads=[("cv", vs), ("y", ys), ("par",)], writes=[("y", ys)])
        S.add("act", lambda e: e.activation(out=szb[:, zs, :], in_=ps[:, b_z, :], func=AF.Silu),
              reads=[("ps", b_z)], writes=[("sz", zs)])
        S.add("dve", lambda e: e.tensor_tensor(out=szb[:, zs, :], in0=szb[:, zs, :], in1=ps[:, b_bg, :],
                                               op=ALU.mult),
              reads=[("sz", zs), ("ps", b_bg)], writes=[("sz", zs)])
        S.add("dve", lambda e: e.tensor_tensor(out=mT[:, c, tsl(j)], in0=szb[:, zs, :], in1=yb[:, ys, :],
                                               op=ALU.mult),
              reads=[("sz", zs), ("y", ys)], writes=[("m", c, j)])

    def stage_B_uz(i, c, j, slot, first_tile, kouter=False):
        bl = i // 2
        w = POOL_WINDOWS[c // 2]
        bu, bz = bank(), bank()
        mm_group([bu, bz], lambda g, k: (win[:, slot, k, g, :], [("win", slot)]), j, kouter)
        zs = nxt("sz", NSZ)
        us = nxt("cv", 2)
        ds = nxt("d", ND)
        S.add("act", lambda e: e.activation(out=cvb[:, us, 1:HB], in_=halo_b[:, bl, c, 0:15], func=AF.Copy),
              reads=[("halo_b", bl, c)], writes=[("cvh", us)])
        S.add("act", lambda e: e.activation(out=cvb[:, us, HB:HB + TW], in_=ps[:, bu, :], func=AF.Copy),
              reads=[("ps", bu)], writes=[("cv", us)])
        S.add("act", lambda e: e.activation(out=halo_b[:, bl, c, 0:15], in_=cvb[:, us, HB + TW - 15:HB + TW],
                                            func=AF.Copy),
              reads=[("cv", us)], writes=[("halo_b", bl, c)])
        S.add("act", lambda e: e.activation(out=szb[:, zs, :], in_=ps[:, bz, :], func=AF.Silu),
              reads=[("ps", bz)], writes=[("sz", zs)])
        nstg = {2: 1, 4: 2, 8: 3, 16: 4}[w]
        prev, prev_keys = (lambda lo, hi: cvb[:, us, lo:hi]), [("cv", us), ("cvh", us)]
        for st_ in range(1, nstg + 1):
            lo = HB - (w - 2 ** st_)
            sh = 2 ** (st_ - 1)
            pp = (st_ - 1) % 2
            S.add("dve", lambda e, lo=lo, sh=sh, pp=pp, prev=prev: e.tensor_tensor(
                out=sab[:, pp, lo:HB + TW], in0=prev(lo, HB + TW), in1=prev(lo - sh, HB + TW - sh), op=ALU.add),
                reads=prev_keys, writes=[("sa", pp)])
            prev, prev_keys = (lambda lo, hi, pp=pp: sab[:, pp, lo:hi]), [("sa", pp)]
        S.add("dve", lambda e, prev=prev: e.scalar_tensor_tensor(
            out=dT[:, ds, :], in0=prev(HB, HB + TW), scalar=1.0 / w, in1=cvb[:, us, HB:HB + TW],
            op0=ALU.mult, op1=ALU.subtract),
            reads=prev_keys + [("cv", us)], writes=[("d", ds)])
        if first_tile:
            wi = POOL_WINDOWS.index(w)
            S.add("dve", lambda e, prev=prev: e.tensor_tensor(out=fix[:, 0:15], in0=prev(HB, HB + 15),
                                                              in1=invc[:, wi, 0:15], op=ALU.mult),
                  reads=prev_keys + [("invcw", wi)], writes=[("fix",)])
            S.add("dve", lambda e: e.tensor_tensor(out=dT[:, ds, 0:15], in0=fix[:, 0:15], in1=cvb[:, us, HB:HB + 15],
                                                   op=ALU.subtract),
                  reads=[("fix",), ("cv", us)], writes=[("d", ds)])
        return ds, zs

    def stage_B_grp(i, g, j, dss, zss):
        bl = i // 2
        for oc in range(2):
            b = bank()
            for ic in range(2):
                S.add("pe", lambda e, oc=oc, ic=ic, b=b: e.matmul(
                    ps[:, b, :], lhsT=wgrp[:, g, ic, oc * 128:(oc + 1) * 128], rhs=dT[:, dss[ic], :],
                    start=(ic == 0), stop=(ic == 1)),
                    reads=[("wgrp",), ("d", dss[ic])], writes=[("ps", b)])
            c = 2 * g + oc
            S.add("dve", lambda e, b=b, c=c, oc=oc: e.scalar_tensor_tensor(
                out=mT[:, c, tsl(j)], in0=ps[:, b, :], scalar=pcol(PC_SCALE + bl * KC + c), in1=szb[:, zss[oc], :],
                op0=ALU.mult, op1=ALU.mult),
                reads=[("ps", b), ("sz", zss[oc]), ("par",)], writes=[("m", c, j)])

    def stage_out(i, j, hooks=None):
        korder = chunk_order(i)
        for o in range(KC):
            b = bank()
            for n_, k in enumerate(korder):
                S.add("pe", lambda e, o=o, k=k, b=b, n_=n_: e.matmul(
                    ps[:, b, :], lhsT=wout[:, k, o * 128:(o + 1) * 128], rhs=mT[:, k, tsl(j)],
                    start=(n_ == 0), stop=(n_ == KC - 1)),
                    reads=[("wout", k // 4), ("m", k, j)], writes=[("ps", b)])
            S.add("dve", lambda e, o=o, b=b: e.tensor_tensor(out=hT[:, o, tsl(j)], in0=hT[:, o, tsl(j)],
                                                            in1=ps[:, b, :], op=ALU.add),
                  reads=[("h", o, j), ("ps", b)], writes=[("h", o, j)])
            if hooks and o in hooks:
                hooks[o]()

    def stage_ple(i, j, hooks=None):
        for o in range(KC):
            bg, bp = bank(), bank()
            for k2 in range(2):
                S.add("pe", lambda e, o=o, k2=k2, bp=bp: e.matmul(ps[:, bp, :],
                                                                 lhsT=wproj[:, k2, o * 128:(o + 1) * 128],
                                                                 rhs=pT[:, j, k2, :], start=(k2 == 0),
                                                                 stop=(k2 == 1)),
                      reads=[("wproj",), ("pT", j)], writes=[("ps", bp)])
            for k in range(KC):
                S.add("pe", lambda e, o=o, k=k, bg=bg: e.matmul(ps[:, bg, :], lhsT=wgate[:, k, o * 128:(o + 1) * 128],
                                                               rhs=hn[:, k, tsl(j)], start=(k == 0),
                                                               stop=(k == KC - 1)),
                      reads=[("wgate", o // 4), ("hn", k, j)], writes=[("ps", bg)])
            ts_ = nxt("t", 2)
            ms = nxt("tmp", 2)
            S.add("act", lambda e, bg=bg, ts_=ts_: e.activation(out=tb[:, ts_, :], in_=ps[:, bg, :], func=AF.Tanh,
                                                               scale=0.5),
                  reads=[("ps", bg)], writes=[("t", ts_)])
            S.add("dve", lambda e, bp=bp, ts_=ts_, ms=ms: e.scalar_tensor_tensor(
                out=tmpb[:, ms, :], in0=tb[:, ts_, :], scalar=1.0, in1=ps[:, bp, :], op0=ALU.add, op1=ALU.mult),
                reads=[("t", ts_), ("ps", bp)], writes=[("tmp", ms)])
            S.add("dve", lambda e, o=o, ms=ms: e.scalar_tensor_tensor(
                out=hT[:, o, tsl(j)], in0=tmpb[:, ms, :], scalar=0.5, in1=hT[:, o, tsl(j)],
                op0=ALU.mult, op1=ALU.add),
                reads=[("tmp", ms), ("h", o, j)], writes=[("h", o, j)])
            if hooks and o in hooks:
                hooks[o]()

    fin = {"o": 0}

    def stage_T(sup, j, blocks=(0, 1, 2, 3)):
        for n in blocks:
            b = bank2()
            pb = pair(b)
            c0 = j * TW + n * 128
            for k in range(KC):
                S.add("pe", lambda e, k=k, pb=pb, c0=c0: e.transpose(
                    out=pb[:, k * 128:(k + 1) * 128], in_=hT[:, k, c0:c0 + 128], identity=ident[:]),
                    reads=[("h", k, j), ("ident",)], writes=[("ps", b), ("ps", b + 1)])
            os_ = fin["o"] % 4
            cc = fin["o"] % 8
            fin["o"] += 1
            S.add("act", lambda e, pb=pb, cc=cc: e.activation(out=junk[:], in_=pb, func=AF.Square,
                                                              accum_out=ssq[:, cc:cc + 1]),
                  reads=[("ps", b), ("ps", b + 1)], writes=[("junk",), ("ssq", cc)])
            S.add("pool", lambda e, cc=cc: e.tensor_tensor(out=rt[:, cc:cc + 1], in0=ssq[:, cc:cc + 1],
                                                           in1=cst[:, 0:1], op=ALU.add),
                  reads=[("ssq", cc), ("cst",)], writes=[("rt", cc)])
            S.add("pool", lambda e, cc=cc: e.tensor_tensor(out=rt[:, cc:cc + 1], in0=rt[:, cc:cc + 1],
                                                           in1=cst[:, 1:2], op=ALU.pow),
                  reads=[("rt", cc), ("cst",)], writes=[("rt", cc)])
            S.add("dve", lambda e, pb=pb, os_=os_, cc=cc: e.scalar_tensor_tensor(
                out=ost[:, os_, :], in0=pb, scalar=rt[:, cc:cc + 1], in1=gbc[:], op0=ALU.mult, op1=ALU.mult),
                reads=[("ps", b), ("ps", b + 1), ("rt", cc), ("gbc",)], writes=ost_keys(os_))
            r0 = sup * SUP + j * TW + n * 128
            S.add("sp", lambda e, os_=os_, r0=r0: e.dma_start(out=y_d[r0:r0 + 128, :], in_=ost[:, os_, :]),
                  reads=ost_keys(os_), writes=[("y", r0)], dma=("ost", os_))

    for blk in range(4):
        load_x_block(0, blk)
    load_p(0, 0)
    ensure_pieces(2)
    loads_wout(0)
    pst_ = {"pidx": 0}

    def piece_done():
        pst_["pidx"] += 1
        ensure_pieces(pst_["pidx"] + 2)

    def mixer_phase(i, sup, n1_tile1, smalls):
        order = chunk_order(i)
        first = (sup == 0)
        tail = lambda: None

        def small():
            if smalls:
                smalls.pop(0)()
        if i % 2 == 0:
            c0, c1 = order[0], order[1]
            s0, s1 = pst_["pidx"] % 3, (pst_["pidx"] + 1) % 3
            small()
            n1_tile1.a()
            stage_A(i, c0, 0, s0, kouter=True)
            n1_tile1.b()
            small()
            stage_A(i, c1, 0, s1)
            stage_A(i, c0, 1, s0, kouter=True)
            piece_done()
            stage_A(i, c1, 1, s1)
            piece_done()
            for c in order[2:]:
                slot = pst_["pidx"] % 3
                small()
                stage_A(i, c, 0, slot)
                stage_A(i, c, 1, slot)
                piece_done()
        else:
            res = {}
            pend = []

            def uz(c, j, slot, kouter=False):
                res[(c, j)] = stage_B_uz(i, c, j, slot, first and j == 0, kouter)
                while len(pend) > 0 and pend[0][2] < uz.n:
                    g, jj, _ = pend.pop(0)
                    emit_grp(g, jj)
                uz.n += 1
                g = c // 2
                if (2 * g, j) in res and (2 * g + 1, j) in res:
                    pend.append((g, j, uz.n))
            uz.n = 0

            def emit_grp(g, j):
                a, b_ = res[(2 * g, j)], res[(2 * g + 1, j)]
                stage_B_grp(i, g, j, (a[0], b_[0]), (a[1], b_[1]))
            c0, c1 = order[0], order[1]
            s0, s1 = pst_["pidx"] % 3, (pst_["pidx"] + 1) % 3
            small()
            n1_tile1.a()
            uz(c0, 0, s0, kouter=True)
            n1_tile1.b()
            small()
            uz(c1, 0, s1)
            uz(c0, 1, s0, kouter=True)
            piece_done()
            uz(c1, 1, s1)
            piece_done()
            for c in order[2:]:
                slot = pst_["pidx"] % 3
                small()
                uz(c, 0, slot)
                uz(c, 1, slot)
                piece_done()
            rest = []
            while pend:
                g, jj, _ = pend.pop(0)
                if jj == 0:
                    emit_grp(g, jj)
                else:
                    rest.append((g, jj))
            tail = lambda: [emit_grp(g, jj) for g, jj in rest]
        while smalls:
            smalls.pop(0)()
        return tail

    for sup in range(nsup):
        if sup == 0:
            stage_X(sup, 0)
            Norm(0, PC_NMIX + 0 * KC).both()
            stage_X(sup, 1)
        if sup == 0:
            dump("x")
        for i in range(DEPTH):
            smalls = small_loads(i, sup)
            for j in range(NT):
                stage_P(i, j)
            mixer_tail = mixer_phase(i, sup, Norm(1, PC_NMIX + i * KC), smalls)
            if sup == 0:
                dump("m%d" % i, "m")
            n2_0 = Norm(0, PC_NPLE + i * KC)
            n2_1 = Norm(1, PC_NPLE + i * KC)
            last = (i == DEPTH - 1)
            more = sup + 1 < nsup
            stage_out(i, 0)
            mixer_tail()
            load_p_next(i, sup)
            stage_out(i, 1, hooks={0: n2_0.a, 3: n2_0.b})
            load_wout_next(i, sup)
            if last and more:
                for blk in range(4):
                    load_x_block(sup + 1, blk)
            if sup == 0:
                dump("mix%d" % i)
            stage_ple(i, 0, hooks={0: n2_1.a, 2: n2_1.b})
            if not last:
                nx0 = Norm(0, PC_NMIX + (i + 1) * KC)
                stage_ple(i, 1, hooks={0: nx0.a, 2: nx0.b})
            else:
                hk = {0: (lambda: stage_T(sup, 0, (0,))), 1: (lambda: stage_T(sup, 0, (1,))),
                      2: (lambda: stage_T(sup, 0, (2,))), 3: (lambda: stage_T(sup, 0, (3,)))}
                if more:
                    hk[4] = lambda: stage_X(sup + 1, 0, (0, 1))
                    hk[5] = lambda: stage_X(sup + 1, 0, (2, 3))
                stage_ple(i, 1, hooks=hk)
            if sup == 0:
                dump("L%d" % i)
        if sup + 1 < nsup:
            n10 = Norm(0, PC_NMIX + 0 * KC)
            n10.a()
            stage_T(sup, 1)
            n10.b()
            stage_X(sup + 1, 1)
        else:
            stage_T(sup, 1)

    S.finalize()
    sems = {}
    for eng in Sched.ENGS:
        sems[("eng", eng)] = es.enter_context(nc.semaphore("c_" + eng))
    for n_, k in enumerate(S.dma_keys):
        sems[("dma", k)] = es.enter_context(nc.semaphore("d%d" % n_))

    with nc.Block() as block:
        @block.tensor
        def _(e):
            S.emit_engine("pe", e, sems)

        @block.scalar
        def _(e):
            S.emit_engine("act", e, sems)

        @block.vector
        def _(e):
            S.emit_engine("dve", e, sems)

        @block.gpsimd
        def _(e):
            S.emit_engine("pool", e, sems)

        @block.sync
        def _(e):
            waited = S.emit_engine("sp", e, sems)
            for k, v in S.final_dma.items():
                if k[0] in ("ost", "dbg") and waited.get(("dma", k), 0) < v:
                    e.wait_ge(sems[("dma", k)], v)
    es.close()
    return nc


def feature_major(v):
    v = np.asarray(v, dtype=np.float32).reshape(-1, KC, 128)
    return np.ascontiguousarray(v.transpose(2, 0, 1).reshape(128, -1))


def make_params(norm_mix, ple_norm, final_norm, a_w_conv, b_scale):
    cols = [feature_major(norm_mix), feature_major(ple_norm), feature_major(final_norm),
            feature_major(a_w_conv), feature_major(b_scale)]
    par = np.ascontiguousarray(np.concatenate(cols, axis=1))
    assert par.shape == (128, NPAR), par.shape
    return par


_NC_CACHE = {}


def kernel(x, p, norm_mix, a_w_in, a_w_conv, a_w_out, b_w_in, b_w_grp, b_scale, b_w_out,
           ple_norm, ple_w_gate, ple_w_proj, final_norm):
    n = 8
    if "nc" not in _NC_CACHE:
        _NC_CACHE["nc"] = build_program()
    nc = _NC_CACHE["nc"]
    par = make_params(norm_mix, ple_norm, final_norm, a_w_conv, b_scale)
    shared = {
        "params": par,
        "gfin": np.ascontiguousarray(np.asarray(final_norm, dtype=np.float32).reshape(1, D)),
        "a_w_in": np.ascontiguousarray(a_w_in, dtype=np.float32),
        "a_w_out": np.ascontiguousarray(a_w_out, dtype=np.float32),
        "b_w_in": np.ascontiguousarray(b_w_in, dtype=np.float32),
        "b_w_grp": np.ascontiguousarray(b_w_grp, dtype=np.float32),
        "b_w_out": np.ascontiguousarray(b_w_out, dtype=np.float32),
        "ple_w_gate": np.ascontiguousarray(ple_w_gate, dtype=np.float32),
        "ple_w_proj": np.ascontiguousarray(ple_w_proj, dtype=np.float32),
    }
    in_maps = []
    for b in range(n):
        m = dict(shared)
        m["x"] = np.ascontiguousarray(x[b], dtype=np.float32)
        m["p"] = np.ascontiguousarray(p[:, b], dtype=np.float32)
        in_maps.append(m)
    res = run_bass_kernel_spmd(nc, in_maps, core_ids=list(range(n)))
    return np.stack([np.asarray(r["y"], dtype=np.float32) for r in res.results], axis=0)
```
